# Optimizing a Trainium2 kernel written in Bass

```python
import jax, jax.numpy as jnp
from jax import lax
import numpy as np

D_MODEL = 1024
BATCH = 8
SEQ = 4096
DEPTH = 2

D_MIX = D_MODEL
CONV_CH = D_MIX // 2
CONV_WIDTH = 31
FOX_HEADS = 8
FOX_HEAD_DIM = 64
FOX_WIDTH = FOX_HEADS * FOX_HEAD_DIM
Q_BLOCK = 128
N_MEM = 256
MEM_HEADS = 4
MEM_HEAD_DIM = 128
MEM_INNER = MEM_HEADS * MEM_HEAD_DIM
D_FF = 4 * D_MODEL
EPS = 1e-6
NEG_INF = -1e30
IN_COLS = 2 * CONV_CH + 3 * FOX_WIDTH + FOX_HEADS

kernel_name = "hymba_conformer_fox_sandwich_memory"


def rms_norm(x, g):
    xf = x.astype(jnp.float32)
    y = xf * lax.rsqrt(jnp.mean(xf * xf, axis=-1, keepdims=True) + EPS)
    return (y * g.astype(jnp.float32)).astype(x.dtype)


def layer_norm(x, g, b):
    xf = x.astype(jnp.float32)
    mu = jnp.mean(xf, axis=-1, keepdims=True)
    xc = xf - mu
    y = xc * lax.rsqrt(jnp.mean(xc * xc, axis=-1, keepdims=True) + EPS)
    return (y * g.astype(jnp.float32) + b.astype(jnp.float32)).astype(x.dtype)


def causal_depthwise_conv(u, w, b):
    out = lax.conv_general_dilated(
        u, w[:, None, :], window_strides=(1,), padding=[(CONV_WIDTH - 1, 0)],
        dimension_numbers=("NWC", "WIO", "NWC"), feature_group_count=u.shape[-1])
    return out + b


def forgetting_attention(q, k, v, log_f):
    S = q.shape[1]
    dh = q.shape[-1]
    scale = dh ** -0.5
    cum = jnp.cumsum(log_f, axis=1).transpose(0, 2, 1)
    outs = []
    for i in range(S // Q_BLOCK):
        q0 = i * Q_BLOCK
        kl = q0 + Q_BLOCK
        qb = q[:, q0:kl]
        kb = k[:, :kl]
        vb = v[:, :kl]
        logits = jnp.einsum("bqhd,bkhd->bhqk", qb, kb,
                            preferred_element_type=jnp.float32) * scale
        logits = logits + cum[:, :, q0:kl, None] - cum[:, :, None, :kl]
        q_pos = q0 + jnp.arange(Q_BLOCK)
        k_pos = jnp.arange(kl)
        mask = k_pos[None, :] <= q_pos[:, None]
        p = jax.nn.softmax(jnp.where(mask, logits, NEG_INF), axis=-1)
        outs.append(jnp.einsum("bhqk,bkhd->bqhd", p.astype(vb.dtype), vb))
    return jnp.concatenate(outs, axis=1)


def hybrid_mixer(h, w_in, b_forget, conv_w, conv_b, conv_ln_g, conv_ln_b, w_out):
    B, S, _ = h.shape
    z = h @ w_in
    o = 0
    a = z[..., o:o + CONV_CH]; o += CONV_CH
    g = z[..., o:o + CONV_CH]; o += CONV_CH
    q = z[..., o:o + FOX_WIDTH]; o += FOX_WIDTH
    k = z[..., o:o + FOX_WIDTH]; o += FOX_WIDTH
    v = z[..., o:o + FOX_WIDTH]; o += FOX_WIDTH
    f_logit = z[..., o:o + FOX_HEADS]

    u = a * jax.nn.sigmoid(g)
    u = causal_depthwise_conv(u, conv_w, conv_b)
    u = jax.nn.silu(layer_norm(u, conv_ln_g, conv_ln_b))

    log_f = jax.nn.log_sigmoid((f_logit + b_forget).astype(jnp.float32))
    shp = (B, S, FOX_HEADS, FOX_HEAD_DIM)
    att = forgetting_attention(q.reshape(shp), k.reshape(shp), v.reshape(shp), log_f)
    att = att.reshape(B, S, FOX_WIDTH)

    return jnp.concatenate([u, att], axis=-1) @ w_out


def memory_cross_attention(h, mem_n, w_mq, w_mk, w_mv, w_mo):
    B, S, _ = h.shape
    q = (h @ w_mq).reshape(B, S, MEM_HEADS, MEM_HEAD_DIM)
    k = (mem_n @ w_mk).reshape(B, N_MEM, MEM_HEADS, MEM_HEAD_DIM)
    v = (mem_n @ w_mv).reshape(B, N_MEM, MEM_HEADS, MEM_HEAD_DIM)
    logits = jnp.einsum("bqhd,bmhd->bhqm", q, k,
                        preferred_element_type=jnp.float32) * (MEM_HEAD_DIM ** -0.5)
    p = jax.nn.softmax(logits, axis=-1)
    out = jnp.einsum("bhqm,bmhd->bqhd", p.astype(v.dtype), v).reshape(B, S, MEM_INNER)
    return out @ w_mo


def squared_relu_mlp(h, w_up, w_down):
    return jnp.square(jax.nn.relu(h @ w_up)) @ w_down


def setup_inputs(seed: int = 0) -> dict:
    key = jax.random.key(seed)
    ks = jax.random.split(key, 24)
    nrm = lambda k, shape, fan_in: jax.random.normal(k, shape, jnp.float32) * (fan_in ** -0.5)
    gain = lambda k, shape: 1.0 + 0.05 * jax.random.normal(k, shape, jnp.float32)
    small = lambda k, shape: 0.02 * jax.random.normal(k, shape, jnp.float32)
    L = DEPTH
    return {
        "x": jax.random.normal(ks[0], (BATCH, SEQ, D_MODEL), jnp.float32),
        "mem": jax.random.normal(ks[1], (BATCH, N_MEM, D_MODEL), jnp.float32),
        "norm_mix_pre": gain(ks[2], (L, D_MODEL)),
        "norm_mix_post": gain(ks[3], (L, D_MODEL)),
        "w_in": nrm(ks[4], (L, D_MODEL, IN_COLS), D_MODEL),
        "b_forget": jax.random.uniform(ks[5], (L, FOX_HEADS), jnp.float32, 1.0, 5.0),
        "conv_w": nrm(ks[6], (L, CONV_WIDTH, CONV_CH), CONV_WIDTH),
        "conv_b": small(ks[7], (L, CONV_CH)),
        "conv_ln_g": gain(ks[8], (L, CONV_CH)),
        "conv_ln_b": small(ks[9], (L, CONV_CH)),
        "w_out": nrm(ks[10], (L, D_MIX, D_MODEL), D_MIX),
        "norm_mem_pre": gain(ks[11], (L, D_MODEL)),
        "norm_mem_post": gain(ks[12], (L, D_MODEL)),
        "norm_memkv": gain(ks[13], (L, D_MODEL)),
        "w_mq": nrm(ks[14], (L, D_MODEL, MEM_INNER), D_MODEL),
        "w_mk": nrm(ks[15], (L, D_MODEL, MEM_INNER), D_MODEL),
        "w_mv": nrm(ks[16], (L, D_MODEL, MEM_INNER), D_MODEL),
        "w_mo": nrm(ks[17], (L, MEM_INNER, D_MODEL), MEM_INNER),
        "norm_mlp_pre": gain(ks[18], (L, D_MODEL)),
        "norm_mlp_post": gain(ks[19], (L, D_MODEL)),
        "w_up": nrm(ks[20], (L, D_MODEL, D_FF), D_MODEL),
        "w_down": nrm(ks[21], (L, D_FF, D_MODEL), D_FF),
    }


def reference(x, mem, norm_mix_pre, norm_mix_post, w_in, b_forget, conv_w, conv_b,
              conv_ln_g, conv_ln_b, w_out, norm_mem_pre, norm_mem_post, norm_memkv,
              w_mq, w_mk, w_mv, w_mo, norm_mlp_pre, norm_mlp_post, w_up, w_down):
    for l in range(DEPTH):
        h = rms_norm(x, norm_mix_pre[l])
        y = hybrid_mixer(h, w_in[l], b_forget[l], conv_w[l], conv_b[l],
                         conv_ln_g[l], conv_ln_b[l], w_out[l])
        x = x + rms_norm(y, norm_mix_post[l])
        h = rms_norm(x, norm_mem_pre[l])
        mem_n = rms_norm(mem, norm_memkv[l])
        y = memory_cross_attention(h, mem_n, w_mq[l], w_mk[l], w_mv[l], w_mo[l])
        x = x + rms_norm(y, norm_mem_post[l])
        h = rms_norm(x, norm_mlp_pre[l])
        y = squared_relu_mlp(h, w_up[l], w_down[l])
        x = x + rms_norm(y, norm_mlp_post[l])
    return x
```

```python
import numpy as np
import ml_dtypes
import concourse.bass as bass
import concourse.mybir as mybir
from concourse.bass_utils import run_bass_kernel_spmd

F32 = mybir.dt.float32
BF16 = mybir.dt.bfloat16
AF = mybir.ActivationFunctionType
ALU = mybir.AluOpType

D = 1024
KC = 8
NT = 512
NMEM = 256
IN_COLS = 2568
DFF = 4096
EPS = 1e-6
ENGINES = ("pe", "act", "dve", "pool", "sp")
NVB = 69 + 124
NV = NVB + 8


class Prog:
    def __init__(self):
        self.ops = []
        self.last_writer = {}
        self.readers = {}
        self.dma_last = {}
        self.last_on_eng = {}
        self.canon = {}

    def op(self, eng, fn, reads=(), writes=(), dma_key=None, extra_deps=()):
        i = len(self.ops)
        reads = [self.canon.get(r, r) for r in reads]
        writes = [self.canon.get(w, w) for w in writes]
        deps = {}
        for r in reads:
            j = self.last_writer.get(r)
            if j is not None:
                deps[j] = True
        for w in writes:
            j = self.last_writer.get(w)
            if j is not None:
                deps.setdefault(j, False)
            seen = set()
            for j in reversed(self.readers.get(w, ())):
                pj = self.ops[j]
                if pj["dma_key"] is None and pj["eng"] != "pool":
                    if pj["eng"] in seen:
                        continue
                    seen.add(pj["eng"])
                deps.setdefault(j, False)
        for j in extra_deps:
            deps[j] = True
        if dma_key is not None:
            j = self.dma_last.get(dma_key)
            if j is not None:
                deps[j] = True
            self.dma_last[dma_key] = i
        deps.pop(i, None)
        for w in writes:
            self.last_writer[w] = i
            self.readers[w] = []
        for r in reads:
            self.readers.setdefault(r, []).append(i)
        self.ops.append(dict(eng=eng, fn=fn, deps=deps, dma_key=dma_key, signal=False))
        if fn is not None:
            self.last_on_eng[eng] = i
        return i

    def barrier(self):
        alld = list(self.last_on_eng.values()) + list(self.dma_last.values())
        for e in ENGINES:
            self.op(e, None, extra_deps=alld)

    def emit(self, sems, dma_sems):
        ops = self.ops
        for i, o in enumerate(ops):
            need = []
            for j, raw in o["deps"].items():
                pj = ops[j]
                if pj["fn"] is None:
                    continue
                if pj["dma_key"] is None and pj["eng"] == o["eng"]:
                    if o["eng"] == "pe":
                        continue
                need.append(j)
                if pj["dma_key"] is None:
                    pj["signal"] = True
            o["need"] = need
        cnt = {e: 0 for e in ENGINES}
        dcnt = {}
        for o in ops:
            if o["fn"] is None:
                continue
            if o["dma_key"] is not None:
                k = o["dma_key"]
                dcnt[k] = dcnt.get(k, 0) + 16
                o["sig"] = (dma_sems[k], dcnt[k])
            elif o["signal"]:
                cnt[o["eng"]] += 1
                o["sig"] = (sems[o["eng"]], cnt[o["eng"]])
        per_eng = {e: [] for e in ENGINES}
        for o in ops:
            per_eng[o["eng"]].append(o)
        self.counts = dict(cnt)
        self.n_instr = {e: len(per_eng[e]) for e in ENGINES}

        def run(eng_name, eng):
            waited = {}
            for o in per_eng[eng_name]:
                req = {}
                for j in o["need"]:
                    s, v = ops[j]["sig"]
                    key = id(s)
                    if v > req.get(key, (None, 0))[1]:
                        req[key] = (s, v)
                for key, (s, v) in req.items():
                    if waited.get(key, 0) >= v:
                        continue
                    eng.wait_ge(s, v)
                    waited[key] = v
                if o["fn"] is None:
                    continue
                ins = o["fn"](eng)
                if o["dma_key"] is not None:
                    ins.then_inc(o["sig"][0], 16)
                elif o["signal"]:
                    ins.then_inc(o["sig"][0], 1)

        return run


class SB:
    def __init__(self, nc, base=16512, limit=229376 - 64):
        self.nc = nc
        self.off = base
        self.limit = limit
        self.n = 0
        self.peak = base

    def alloc(self, name, shape, dt):
        esz = 2 if dt == BF16 else 4
        nbytes = esz
        for s in shape[1:]:
            nbytes *= s
        self.off = (self.off + 63) // 64 * 64
        assert self.off + nbytes <= self.limit, (name, self.off, nbytes, self.limit)
        self.n += 1
        t = self.nc.alloc_sbuf_tensor_at("%s_%d" % (name, self.n), list(shape), dt, offset=self.off)
        self.offs = getattr(self, "offs", {})
        self.offs[name] = self.off
        self.off += nbytes
        self.peak = max(self.peak, self.off)
        return t

    def mark(self):
        return self.off

    def release(self, m):
        self.off = m


def build(S, L, dbg=False):
    NTILES = S // NT
    NBLK = S // 128
    nc = bass.Bass("TRN2", target_bir_lowering=False)
    P = Prog()

    def din(name, shape, dt=F32):
        return nc.dram_tensor(name, list(shape), dt, kind="ExternalInput").ap()

    xT = din("xT", [D, S])
    memT = din("memT", [D, NMEM])
    w_in = din("w_in", [L, D, IN_COLS])
    w_out = din("w_out", [L, D, D])
    w_mq = din("w_mq", [L, D, 512])
    w_mk = din("w_mk", [L, D, 512])
    w_mv = din("w_mv", [L, D, 512])
    w_mo = din("w_mo", [L, 512, D])
    w_up = din("w_up", [L, D, DFF])
    w_down = din("w_down", [L, DFF, D])
    vecs = din("vecs", [128, L, NV])
    cst = din("cst", [128, 384])
    yT = nc.dram_tensor("yT", [D, S], F32, kind="ExternalOutput").ap()
    kind_dbg = "ExternalOutput" if dbg else "Internal"
    xa = nc.dram_tensor("xa", [D, S], F32, kind=kind_dbg).ap()
    xb = nc.dram_tensor("xb", [D, S], F32, kind=kind_dbg).ap()
    Qm = nc.dram_tensor("Qm", [512, S], BF16, kind=kind_dbg).ap()
    Km = nc.dram_tensor("Km", [512, S], BF16, kind=kind_dbg).ap()
    Qd = nc.dram_tensor("Qd", [8, 6, S], BF16, kind=kind_dbg).ap()
    Kd = nc.dram_tensor("Kd", [8, 6, S], BF16, kind=kind_dbg).ap()
    uTd = nc.dram_tensor("uTd", [512, S], BF16, kind=kind_dbg).ap()

    sb = SB(nc)
    vec = sb.alloc("vec", [128, L, NV], F32)
    cstf = sb.alloc("cstf", [128, 384], F32)
    identb = sb.alloc("identb", [128, 128], BF16)
    maskb = sb.alloc("maskb", [128, 128], BF16)
    onesD = sb.alloc("onesD", [128, 128], BF16)
    onesC = sb.alloc("onesC", [128, 128], BF16)
    negb = sb.alloc("negb", [8, L], F32)
    epsc = sb.alloc("epsc", [128, 1], F32)
    onec = sb.alloc("onec", [128, 1], F32)
    mark_c = sb.mark()
    xt0 = sb.alloc("xt0", [128, KC, NT], F32)
    sq = sb.alloc("sq", [128, KC, NT], BF16)
    rstd = sb.alloc("rstd", [128, NT], F32)
    yraw = sb.alloc("yraw", [128, KC, NT], F32)
    mark_p2 = sb.mark()
    hT1 = sb.alloc("hT", [128, KC, NT], BF16)
    xt = [xt0, sb.alloc("xt1", [128, KC, NT], F32)]
    Vaug = sb.alloc("Vaug", [128, NBLK, 8, 65], BF16)
    base_mark = sb.mark()
    cur = {"hT": hT1, "offload": False}

    pA = nc.alloc_psum_tensor("pA", [128, NT], F32)
    pB = [nc.alloc_psum_tensor("pB%d" % i, [128, NT], F32) for i in range(4)]
    pO = [nc.alloc_psum_tensor("pO%d" % i, [128, NT], F32) for i in range(2)]
    pT = nc.alloc_psum_tensor("pT", [128, 2 * NT], BF16)
    bstate = {"i": 0}

    def nextB():
        i = bstate["i"] % 4
        bstate["i"] += 1
        return pB[i], ("pB", i)

    dma_keys = set()
    wq = {"i": 0}

    def dma(eng, out, in_, reads, writes, key):
        dma_keys.add(key)
        return P.op(eng, lambda e: e.dma_start(out=out, in_=in_), reads=reads, writes=writes, dma_key=key)

    def load_w(dst, src, K, C, wkey, step=1024, kstep=None, after=(), nkeys=4):
        srcv = src.rearrange("(k p) c -> p k c", p=128)
        if kstep is not None:
            for k0 in range(0, K, kstep):
                key = "wl%d" % (wq["i"] % nkeys)
                wq["i"] += 1
                dma("pool", dst[:, k0:k0 + kstep, :], srcv[:, k0:k0 + kstep, :], list(after), [(wkey, k0 // kstep)], key)
            return
        for c0 in range(0, C, step):
            c1 = min(C, c0 + step)
            key = "wl%d" % (wq["i"] % nkeys)
            wq["i"] += 1
            dma("pool", dst[:, :, c0:c1], srcv[:, :, c0:c1], list(after), [(wkey, c0 // step)], key)

    dma("sp", vec[:], vecs, [], ["vec"], "ld_misc")
    dma("sp", cstf[:], cst, [], ["cstf"], "ld_misc2")
    P.op("dve", lambda e: e.tensor_copy(out=identb[:], in_=cstf[:, 0:128]), reads=["cstf"], writes=["identb"])
    P.op("dve", lambda e: e.tensor_copy(out=maskb[:], in_=cstf[:, 128:256]), reads=["cstf"], writes=["maskb"])
    P.op("dve", lambda e: e.memset(onesD[:], 1.0 / D), writes=["onesD"])
    P.op("dve", lambda e: e.memset(onesC[:], 1.0 / 512), writes=["onesC"])
    P.op("dve", lambda e: e.memset(epsc[:], EPS), writes=["epsc"])
    P.op("dve", lambda e: e.memset(onec[:], 1.0), writes=["onec"])
    P.op("dve", lambda e: e.tensor_scalar(out=negb[:], in0=vec[0:8, :, 68], scalar1=-1.0, scalar2=None, op0=ALU.mult),
         reads=["vec"], writes=["negb"])

    ones3 = sb.alloc("ones3", [8, 3, S], BF16)
    sb.release(base_mark)
    P.op("pool", lambda e: e.memset(ones3[:], 1.0), writes=["ones3"])
    dma("sp", Qd[:, 3:6, :], ones3[:], ["ones3"], [("scr", "qd1")], "dst0")
    dma("sp", Kd[:, 0:3, :], ones3[:], ["ones3"], [("scr", "kd1")], "dst1")

    def gcol(l, base, k):
        return vec[:, l, base + k:base + k + 1]

    def prenorm(xtile, xkey, l, gbase, ncols=NT):
        for kc in range(KC):
            if cur["offload"]:
                P.op("pool", lambda e, kc=kc: e.tensor_tensor(out=sq[:, kc, 0:ncols], in0=xtile[:, kc, 0:ncols],
                                                              in1=xtile[:, kc, 0:ncols], op=ALU.mult),
                     reads=[(xkey, kc)], writes=[("sq", kc)])
            else:
                P.op("act", lambda e, kc=kc: e.activation(out=sq[:, kc, 0:ncols], in_=xtile[:, kc, 0:ncols], func=AF.Square),
                     reads=[(xkey, kc)], writes=[("sq", kc)])
            P.op("pe", lambda e, kc=kc: e.matmul(pA[:, 0:ncols], lhsT=onesD[:], rhs=sq[:, kc, 0:ncols],
                                                 start=(kc == 0), stop=(kc == KC - 1)),
                 reads=[("sq", kc), "onesD"], writes=["pA"])
        P.op("act", lambda e: e.activation(out=rstd[:, 0:ncols], in_=pA[:, 0:ncols], func=AF.Ln, bias=epsc[:, 0:1]),
             reads=["pA", "epsc"], writes=["rstd"])
        P.op("act", lambda e: e.activation(out=rstd[:, 0:ncols], in_=rstd[:, 0:ncols], func=AF.Exp, scale=-0.5),
             reads=["rstd"], writes=["rstd"])
        hT = cur["hT"]
        for kc in range(KC):
            P.op("dve", lambda e, kc=kc: e.scalar_tensor_tensor(
                out=hT[:, kc, 0:ncols], in0=xtile[:, kc, 0:ncols], scalar=gcol(l, gbase, kc), in1=rstd[:, 0:ncols],
                op0=ALU.mult, op1=ALU.mult), reads=[(xkey, kc), "rstd", "vec"], writes=[("hT", kc)])

    def postnorm_residual(xtile, xkey, l, gbase):
        for kc in range(KC):
            P.op("pe", lambda e, kc=kc: e.matmul(pA[:, :], lhsT=onesD[:], rhs=sq[:, kc, :],
                                                 start=(kc == 0), stop=(kc == KC - 1)),
                 reads=[("sq", kc), "onesD"], writes=["pA"])
        P.op("act", lambda e: e.activation(out=rstd[:], in_=pA[:], func=AF.Ln, bias=epsc[:, 0:1]),
             reads=["pA", "epsc"], writes=["rstd"])
        P.op("act", lambda e: e.activation(out=rstd[:], in_=rstd[:], func=AF.Exp, scale=-0.5),
             reads=["rstd"], writes=["rstd"])
        for kc in range(KC):
            P.op("dve", lambda e, kc=kc: e.scalar_tensor_tensor(
                out=yraw[:, kc, :], in0=yraw[:, kc, :], scalar=gcol(l, gbase, kc), in1=rstd[:],
                op0=ALU.mult, op1=ALU.mult), reads=[("yraw", kc), "rstd", "vec"], writes=[("yraw", kc)])
            P.op("pool" if kc < 5 else "dve", lambda e, kc=kc: e.tensor_tensor(
                out=xtile[:, kc, :], in0=xtile[:, kc, :], in1=yraw[:, kc, :], op=ALU.add),
                reads=[(xkey, kc), ("yraw", kc)], writes=[(xkey, kc)])

    def ckeys(xkey):
        return [(xkey, k_) for k_ in range(KC)]

    def proj_fm(wt, wkey, nk, col0, rhs_of, rkeys, oc_list, l, evac):
        for oc in oc_list:
            bank, bkey = nextB()
            for k in range(nk):
                P.op("pe", lambda e, k=k, oc=oc, bank=bank: e.matmul(
                    bank[:, :], lhsT=wt[:, k, col0 + oc * 128:col0 + (oc + 1) * 128], rhs=rhs_of(k),
                    start=(k == 0), stop=(k == nk - 1)), reads=[wkey(k, oc) if callable(wkey) else (wkey, 0), rkeys(k)], writes=[bkey])
            evac(oc, bank, bkey)

    def evac_y(oc, bank, bkey):
        if cur["offload"]:
            P.op("dve", lambda e: e.tensor_copy(out=yraw[:, oc, :], in_=bank[:, :]), reads=[bkey], writes=[("yraw", oc)])
            P.op("pool", lambda e: e.tensor_tensor(out=sq[:, oc, :], in0=yraw[:, oc, :], in1=yraw[:, oc, :], op=ALU.mult),
                 reads=[("yraw", oc)], writes=[("sq", oc)])
            return
        P.op("act", lambda e: e.activation(out=yraw[:, oc, :], in_=bank[:, :], func=AF.Copy),
             reads=[bkey], writes=[("yraw", oc)])
        P.op("act", lambda e: e.activation(out=sq[:, oc, :], in_=bank[:, :], func=AF.Square),
             reads=[bkey], writes=[("sq", oc)])

    def layer(l, xsrc):
        last = (l == L - 1)
        P.barrier()
        sb.release(base_mark)
        P.canon = {}
        cur["hT"] = hT1
        hT = hT1
        P.op("dve", lambda e: e.memset(Vaug[:], 1.0), writes=["Vaug"])
        win = sb.alloc("win", [128, KC, IN_COLS], BF16)
        u = sb.alloc("u", [128, 4, NT + 30], BF16)
        halo = sb.alloc("halo", [128, 4, 30], BF16)
        dg = sb.alloc("dg", [128, 124, 128], BF16)
        acc = yraw
        sg1 = sb.alloc("sg", [128, NT], F32)
        sg = [sg1, sg1]
        qk = [sb.alloc("qk%d" % i, [128, NT], BF16) for i in range(2)]
        uo = sb.alloc("uo", [128, 4, NT], BF16)
        cbf = uo
        csq = nc.alloc_sbuf_tensor_at("csq_%d" % l, [128, 4, NT], BF16, offset=sb.offs["yraw"] + 4 * NT * 4)
        rl = sb.alloc("rl", [128, NT], F32)
        nmr = sb.alloc("nmr", [128, NT], F32)
        m2 = nmr
        ft = sb.alloc("ft", [128, 4, 8], F32)
        cs = [sb.alloc("cs%d" % i, [8, NT], F32) for i in range(2)]
        r1 = sb.alloc("r1", [8, NT], F32)
        DQ = sb.alloc("DQ", [8, 3, NT], BF16)
        DK = sb.alloc("DK", [8, 3, NT], BF16)
        load_w(win, w_in[l], KC, IN_COLS, "win", nkeys=1)
        P.op("pool", lambda e: e.memset(u[:, :, 0:30], 0.0), writes=["u_halo"])
        CW = 69
        pending = []
        pendingB = []

        def build_dg():
            for idx in range(124):
                P.op("dve", lambda e, idx=idx: e.tensor_scalar(
                    out=dg[:, idx, :], in0=identb[:], scalar1=vec[:, l, CW + idx:CW + idx + 1], scalar2=None, op0=ALU.mult),
                    reads=["identb", "vec"], writes=[("dg", idx)])
        def xload(jj, src):
            dma("sp", xt[jj % 2][:], src.rearrange("(k p) t -> p k t", p=128)[:, :, jj * NT:(jj + 1) * NT], [],
                ckeys(("xt", jj % 2)), "xl%d" % (jj % 2))
        xload(0, xsrc)
        for j in range(NTILES):
            T0 = j * NT
            s = j % 2
            xk = ("xt", s)
            if j + 1 < NTILES:
                xload(j + 1, xsrc)
            if j == 0:
                prenorm(xt[s], xk, l, 0)
            hkeys = lambda k: ("hT", k)
            hrhs = lambda k: hT[:, k, :]
            for cc in range(4):
                bank_a, ka = nextB()
                for k in range(KC):
                    P.op("pe", lambda e, k=k, cc=cc, bank=bank_a: e.matmul(
                        bank[:, :], lhsT=win[:, k, cc * 128:(cc + 1) * 128], rhs=hT[:, k, :],
                        start=(k == 0), stop=(k == KC - 1)), reads=[("win", 0), ("hT", k)], writes=[ka])
                bank_g, kg = nextB()
                for k in range(KC):
                    P.op("pe", lambda e, k=k, cc=cc, bank=bank_g: e.matmul(
                        bank[:, :], lhsT=win[:, k, 512 + cc * 128:512 + (cc + 1) * 128], rhs=hT[:, k, :],
                        start=(k == 0), stop=(k == KC - 1)), reads=[("win", 0), ("hT", k)], writes=[kg])
                sgt = sg[cc % 2]
                P.op("act", lambda e, bank=bank_g, sgt=sgt: e.activation(out=sgt[:], in_=bank[:, :], func=AF.Sigmoid),
                     reads=[kg], writes=["sg"])
                P.op("dve", lambda e, bank=bank_a, sgt=sgt, cc=cc: e.tensor_tensor(
                    out=u[:, cc, 30:30 + NT], in0=bank[:, :], in1=sgt[:], op=ALU.mult),
                    reads=[ka, "sg"], writes=[("u", cc)])
            for which, col0, dst in (("q", 1024, Qm), ("k", 1536, Km)):
                for c in range(4):
                    bank, bk = nextB()
                    for k in range(KC):
                        P.op("pe", lambda e, k=k, c=c, bank=bank, col0=col0: e.matmul(
                            bank[:, :], lhsT=win[:, k, col0 + c * 128:col0 + (c + 1) * 128], rhs=hT[:, k, :],
                            start=(k == 0), stop=(k == KC - 1)), reads=[("win", 1), ("hT", k)], writes=[bk])
                    qs_ = qk[c % 2]
                    sc = 0.125 if which == "q" else 1.0
                    P.op("act", lambda e, bank=bank, qs_=qs_, sc=sc: e.activation(out=qs_[:], in_=bank[:, :], func=AF.Copy, scale=sc),
                         reads=[bk], writes=[("qk", c % 2)])
                    dma("sp", dst[c * 128:(c + 1) * 128, T0:T0 + NT], qs_[:], [("qk", c % 2)], [("scr", which)], "qkst%d" % (c % 2))
            if pending:
                pending.pop(0)()
            for sbk in range(4):
                bank, bk = nextB()
                for k in range(KC):
                    P.op("pe", lambda e, k=k, sbk=sbk, bank=bank: e.matmul(
                        bank[:, :], lhsT=hT[:, k, sbk * 128:(sbk + 1) * 128], rhs=win[:, k, 2048:2560],
                        start=(k == 0), stop=(k == KC - 1)), reads=[("win", 2), ("hT", k)], writes=[bk])
                blk = 4 * j + sbk
                P.op("act", lambda e, bank=bank, blk=blk: e.activation(
                    out=Vaug[:, blk, :, 0:64], in_=bank[:, :].rearrange("p (h d) -> p h d", h=8), func=AF.Copy),
                    reads=[bk], writes=["Vaug"])
            Ucum = cstf[:, 256:384]
            for sbk in range(4):
                bank, bk = nextB()
                for k in range(KC):
                    P.op("pe", lambda e, k=k, sbk=sbk, bank=bank: e.matmul(
                        bank[:, 0:8], lhsT=hT[:, k, sbk * 128:(sbk + 1) * 128], rhs=win[:, k, 2560:2568],
                        start=(k == 0), stop=(k == KC - 1)), reads=[("win", 2), ("hT", k)], writes=[bk])
                P.op("dve", lambda e, sbk=sbk, bank=bank: e.tensor_tensor(
                    out=ft[:, sbk, :], in0=bank[:, 0:8], in1=vec[:, l, NVB:NVB + 8], op=ALU.add),
                    reads=[bk, "vec"], writes=["ft"])
            def cs_part(j=j, T0=T0):
                P.op("act", lambda e: e.activation(out=ft[:], in_=ft[:], func=AF.Exp, scale=-1.0), reads=["ft"], writes=["ft"])
                P.op("act", lambda e: e.activation(out=ft[:], in_=ft[:], func=AF.Ln, bias=onec[:, 0:1]), reads=["ft", "onec"], writes=["ft"])
                csn, csp = cs[j % 2], cs[(j + 1) % 2]
                for sbk in range(4):
                    bank, bk = nextB()
                    P.op("pe", lambda e, sbk=sbk, bank=bank: e.matmul(
                        bank[0:8, 0:128], lhsT=ft[:, sbk, :], rhs=Ucum, start=True, stop=True),
                        reads=["ft", "cstf"], writes=[bk])
                    if sbk == 0:
                        carry = 0.0 if j == 0 else csp[:, NT - 1:NT]
                    else:
                        carry = csn[:, sbk * 128 - 1:sbk * 128]
                    P.op("dve", lambda e, sbk=sbk, bank=bank, carry=carry, csn=csn: e.tensor_scalar(
                        out=csn[:, sbk * 128:(sbk + 1) * 128], in0=bank[0:8, 0:128], scalar1=carry, scalar2=None, op0=ALU.add),
                        reads=[bk, ("cs", (j + 1) % 2), ("cs", j % 2)], writes=[("cs", j % 2)])
                ck = ("cs", j % 2)
                P.op("dve", lambda e, csn=csn: e.tensor_copy(out=DK[:, 0, :], in_=csn[:]), reads=[ck], writes=["DK"])
                P.op("dve", lambda e, csn=csn: e.tensor_tensor(out=r1[:], in0=csn[:], in1=DK[:, 0, :], op=ALU.subtract), reads=[ck, "DK"], writes=["r1"])
                P.op("dve", lambda e: e.tensor_copy(out=DK[:, 1, :], in_=r1[:]), reads=["r1"], writes=["DK"])
                P.op("dve", lambda e: e.tensor_tensor(out=r1[:], in0=r1[:], in1=DK[:, 1, :], op=ALU.subtract), reads=["r1", "DK"], writes=["r1"])
                P.op("dve", lambda e: e.tensor_copy(out=DK[:, 2, :], in_=r1[:]), reads=["r1"], writes=["DK"])
                P.op("dve", lambda e: e.tensor_scalar(out=DQ[:], in0=DK[:], scalar1=-1.0, scalar2=None, op0=ALU.mult),
                     reads=["DK"], writes=["DQ"])
                dma("sp", Qd[:, 0:3, T0:T0 + NT], DQ[:], ["DQ"], [("scr", "qd")], "dst0")
                dma("sp", Kd[:, 3:6, T0:T0 + NT], DK[:], ["DK"], [("scr", "kd")], "dst1")
            if j + 1 < NTILES:
                prenorm(xt[(j + 1) % 2], ("xt", (j + 1) % 2), l, 0)
            if pendingB:
                pendingB.pop(0)()
            if j == 0:
                build_dg()
            if j > 0:
                P.op("pool", lambda e: e.tensor_copy(out=u[:, :, 0:30], in_=halo[:]), reads=["halo"], writes=["u_halo"])
            for cc in range(4):
                bank, bk = nextB()
                for k in range(31):
                    P.op("pe", lambda e, cc=cc, k=k, bank=bank: e.matmul(
                        bank[:, :], lhsT=dg[:, cc * 31 + k, :], rhs=u[:, cc, k:k + NT], start=(k == 0), stop=(k == 30)),
                        reads=[("u", cc), "u_halo", ("dg", cc * 31 + k)], writes=[bk])
                P.op("act", lambda e, cc=cc, bank=bank: e.activation(
                    out=acc[:, cc, :], in_=bank[:, :], func=AF.Identity, bias=vec[:, l, 56 + cc:57 + cc]),
                    reads=[bk, "vec"], writes=[("acc", cc)])
            cs_part()
            P.op("pool", lambda e: e.tensor_copy(out=halo[:], in_=u[:, :, NT:NT + 30]),
                 reads=[("u", 0), ("u", 1), ("u", 2), ("u", 3)], writes=["halo"])
            def ln_part(T0=T0):
                acck = [("acc", c) for c in range(4)]
                P.op("act", lambda e: e.activation(out=csq[:], in_=acc[:, 0:4, :], func=AF.Square), reads=acck, writes=["csq"])
                P.op("dve", lambda e: e.tensor_copy(out=cbf[:], in_=acc[:, 0:4, :]), reads=acck, writes=["uo"])
                for cc in range(4):
                    P.op("pe", lambda e, cc=cc: e.matmul(pO[0][:, :], lhsT=onesC[:], rhs=cbf[:, cc, :], start=(cc == 0), stop=(cc == 3)),
                         reads=["uo", "onesC"], writes=["pO0"])
                for cc in range(4):
                    P.op("pe", lambda e, cc=cc: e.matmul(pO[1][:, :], lhsT=onesC[:], rhs=csq[:, cc, :], start=(cc == 0), stop=(cc == 3)),
                         reads=["csq", "onesC"], writes=["pO1"])
                P.op("act", lambda e: e.activation(out=m2[:], in_=pO[0][:, :], func=AF.Square), reads=["pO0"], writes=["nmr"])
                P.op("dve", lambda e: e.tensor_tensor(out=rl[:], in0=pO[1][:, :], in1=m2[:], op=ALU.subtract), reads=["pO1", "nmr"], writes=["rl"])
                P.op("act", lambda e: e.activation(out=rl[:], in_=rl[:], func=AF.Ln, bias=epsc[:, 0:1]), reads=["rl", "epsc"], writes=["rl"])
                P.op("act", lambda e: e.activation(out=rl[:], in_=rl[:], func=AF.Exp, scale=-0.5), reads=["rl"], writes=["rl"])
                P.op("dve", lambda e: e.scalar_tensor_tensor(out=nmr[:], in0=pO[0][:, :], scalar=-1.0, in1=rl[:], op0=ALU.mult, op1=ALU.mult),
                     reads=["pO0", "rl"], writes=["nmr"])

            def ln_part_b(T0=T0):
                for cc in range(4):
                    P.op("dve", lambda e, cc=cc: e.tensor_tensor(out=acc[:, cc, :], in0=acc[:, cc, :], in1=rl[:], op=ALU.mult),
                         reads=[("acc", cc), "rl"], writes=[("acc", cc)])
                    P.op("dve", lambda e, cc=cc: e.tensor_tensor(out=acc[:, cc, :], in0=acc[:, cc, :], in1=nmr[:], op=ALU.add),
                         reads=[("acc", cc), "nmr"], writes=[("acc", cc)])
                    P.op("act", lambda e, cc=cc: e.activation(out=uo[:, cc, :], in_=acc[:, cc, :], func=AF.Silu,
                                                              scale=vec[:, l, 60 + cc:61 + cc], bias=vec[:, l, 64 + cc:65 + cc]),
                         reads=[("acc", cc), "vec"], writes=["uo"])
                dma("sp", uTd.rearrange("(c p) t -> p c t", p=128)[:, :, T0:T0 + NT], uo[:], ["uo"], [("scr", "u")], "ust")
            pending.append(ln_part)
            pendingB.append(ln_part_b)

        while pending:
            pending.pop(0)()
        while pendingB:
            pendingB.pop(0)()

        P.barrier()
        sb.release(base_mark)
        wout = sb.alloc("wout", [128, KC, D], BF16)
        wmq = sb.alloc("wmq", [128, KC, 512], BF16)
        wmo = sb.alloc("wmo", [128, 4, D], BF16)
        memKT = sb.alloc("memKT", [128, 4, NMEM], BF16)
        memV = sb.alloc("memV", [128, 2, 4, 129], BF16)
        kst = [sb.alloc("kst%d" % i, [128, S], BF16) for i in range(2)]
        qtile = sb.alloc("qtile", [128, 8, NT], BF16)
        catT = sb.alloc("catT", [128, 8, NT], BF16)
        att = sb.alloc("att", [128, 4, 512], BF16)
        pt = [sb.alloc("pt%d" % i, [128, NT], BF16) for i in range(4)]
        rden = sb.alloc("rden", [128, 4], F32)
        qm = sb.alloc("qm", [128, 4, NT], BF16)
        m1b = sb.mark()
        wmk = sb.alloc("wmk", [128, KC, 512], BF16)
        wmv = sb.alloc("wmv", [128, KC, 512], BF16)
        def mem_prep():
            mk_ = "yrawm"
            dma("sp", yraw[:, :, 0:NMEM], memT.rearrange("(k p) t -> p k t", p=128), [], ckeys(mk_) + [("yraw", k_) for k_ in range(KC)], "ld_misc")
            prenorm(yraw, mk_, l, 32, ncols=NMEM)
            P.op("dve", lambda e: e.memset(memV[:], 1.0), writes=["memV"])
            for hd in range(4):
                bank, bk = nextB()
                for k in range(KC):
                    P.op("pe", lambda e, k=k, hd=hd, bank=bank: e.matmul(
                        bank[:, 0:NMEM], lhsT=wmk[:, k, hd * 128:(hd + 1) * 128], rhs=hT[:, k, 0:NMEM],
                        start=(k == 0), stop=(k == KC - 1)), reads=[("wmk", 0), ("hT", k)], writes=[bk])
                P.op("act", lambda e, hd=hd, bank=bank: e.activation(out=memKT[:, hd, :], in_=bank[:, 0:NMEM], func=AF.Copy, scale=128.0 ** -0.5),
                     reads=[bk], writes=["memKT"])
            for mc in range(2):
                bank, bk = nextB()
                for k in range(KC):
                    P.op("pe", lambda e, k=k, mc=mc, bank=bank: e.matmul(
                        bank[:, :], lhsT=hT[:, k, mc * 128:(mc + 1) * 128], rhs=wmv[:, k, :],
                        start=(k == 0), stop=(k == KC - 1)), reads=[("wmv", 0), ("hT", k)], writes=[bk])
                P.op("act", lambda e, mc=mc, bank=bank: e.activation(
                    out=memV[:, mc, :, 0:128], in_=bank[:, :].rearrange("p (h d) -> p h d", h=4), func=AF.Copy),
                    reads=[bk], writes=["memV"])

        xdst = xa
        ptc = {"i": 0}
        seq = [(jj, hh) for jj in range(NTILES) for hh in range(8)]

        pre0 = (S >= 8 * NT)

        def kload(idx):
            jj, hh = seq[idx]
            if jj == 0 and pre0:
                return
            kl_ = (jj + 1) * NT
            dma("sp", kst[hh % 2][0:64, 0:kl_], Km[hh * 64:(hh + 1) * 64, 0:kl_], [("scr", "k")], [("kst", hh % 2)], "kl%d" % (hh % 2))
            dma("sp", kst[hh % 2][64:70, 0:kl_], Kd[hh, :, 0:kl_], [("scr", "kd")], [("kst", hh % 2)], "kd%d" % (hh % 2))
        def uload(jj):
            dma("sp", catT[:, 0:4, :], uTd.rearrange("(c p) t -> p c t", p=128)[:, :, jj * NT:(jj + 1) * NT], [("scr", "u")], ["catT_u"], "ul")

        qtile2 = nc.alloc_sbuf_tensor_at("qtile2_%d" % l, [128, 8, NT], BF16, offset=sb.offs["wmv"])
        qts = [qtile, qtile2]

        def qslot(jj):
            return 1 if (jj >= 3 and jj % 2 == 1) else 0

        def qload(jj):
            qs_ = qslot(jj)
            qt_ = qts[qs_]
            wr = [("qtile", qs_)] + ([("wmv", 0)] if qs_ == 1 else [])
            dma("sp", qt_[0:64, :, :], Qm.rearrange("(h r) t -> r h t", r=64)[:, :, jj * NT:(jj + 1) * NT], [("scr", "q")], wr, "ql0")
            dma("sp", qt_[64:70, :, :], Qd.rearrange("h r t -> r h t")[:, :, jj * NT:(jj + 1) * NT], [("scr", "qd")], wr, "ql1")
        qload(0)
        if pre0:
            dma("sp", kst[0][0:64, 0:8 * NT].rearrange("r (h t) -> r h t", h=8),
                Km.rearrange("(h r) t -> r h t", r=64)[:, :, 0:NT], [("scr", "k")], [("kst", 0)], "kl0")
            dma("sp", kst[0][64:70, 0:8 * NT].rearrange("r (h t) -> r h t", h=8),
                Kd.rearrange("h r t -> r h t")[:, :, 0:NT], [("scr", "kd")], [("kst", 0)], "kd0")
        else:
            kload(0)
        uload(0)
        xload(0, xsrc)
        def wloads(stage):
            if stage == 0:
                load_w(wout, w_out[l], KC, D, "wout", after=[("qtile", 0), ("kst", 0), ("kst", 1)])
            elif stage == 1:
                load_w(wmq, w_mq[l], KC, 512, "wmq", after=[("kst", 0), ("kst", 1)])
            else:
                load_w(wmk, w_mk[l], KC, 512, "wmk", after=[("kst", 0), ("kst", 1)])
                load_w(wmv, w_mv[l], KC, 512, "wmv", after=[("kst", 0), ("kst", 1)])
                load_w(wmo, w_mo[l], 4, D, "wmo", after=[("kst", 0), ("kst", 1)])
        cur["offload"] = True
        otok = nc.alloc_sbuf_tensor_at("otok_%d" % l, [128, 4, 512], BF16, offset=sb.offs["wmk"])
        oT = nc.alloc_sbuf_tensor_at("oT_%d" % l, [128, 4, NT], BF16, offset=sb.offs["wmk"] + 4096)
        rden2 = sb.alloc("rden2", [128, 4], F32)
        ovs = [pO[i][:, 0:258].rearrange("p (q c) -> p q c", q=2) for i in range(2)]
        LOOK = 2
        if NTILES > 1:
            xload(1, xsrc)

        def transposes(src, dst_of, dkey_of):
            for cc in range(4):
                for qs in range(4):
                    P.op("pe", lambda e, cc=cc, qs=qs: e.transpose(
                        out=pT[:, qs * 128:(qs + 1) * 128], in_=src[:, qs, cc * 128:(cc + 1) * 128], identity=identb[:]),
                        reads=["att" if src is att else "otok", "identb"], writes=["pT"])
                P.op("dve", lambda e, cc=cc: e.tensor_copy(out=dst_of(cc), in_=pT[:, 0:NT]), reads=["pT"], writes=[dkey_of(cc)])

        def make_tail(j):
            s_ = j % 2
            xk = ("xt", s_)
            T0 = j * NT

            def s1():
                ck_ = lambda k: "catT_u" if k < 4 else ("catT", k - 4)
                proj_fm(wout, "wout", KC, 0, lambda k: catT[:, k, :], ck_, range(KC), l, evac_y)
                if j + 1 < NTILES:
                    uload(j + 1)
                if j >= 1 and j + 1 < NTILES:
                    xload(j + 1, xsrc)

            def s2():
                postnorm_residual(xt[s_], xk, l, 8)

            def s3():
                prenorm(xt[s_], xk, l, 16)

            def s4():
                def evac_qm(oc, bank, bkey):
                    P.op("dve", lambda e: e.tensor_copy(out=qm[:, oc, :], in_=bank[:, :]), reads=[bkey], writes=["qm"])
                proj_fm(wmq, "wmq", KC, 0, lambda k: hT[:, k, :], lambda k: ("hT", k), range(4), l, evac_qm)

            def s5():
                if j == 0:
                    mem_prep()
                msteps = [(hd, mc) for hd in range(4) for mc in range(2)]
                minfo = {}

                def emit_mS(i):
                    hd, mc = msteps[i]
                    bank, bk = nextB()
                    P.op("pe", lambda e, bank=bank, hd=hd, mc=mc: e.matmul(
                        bank[:, :], lhsT=memKT[:, hd, mc * 128:(mc + 1) * 128], rhs=qm[:, hd, :], start=True, stop=True),
                        reads=["memKT", "qm"], writes=[bk])
                    pi = ptc["i"] % 4
                    ptc["i"] += 1
                    ptt = pt[pi]
                    P.op("act", lambda e, bank=bank, ptt=ptt: e.activation(out=ptt[:], in_=bank[:, :], func=AF.Exp),
                         reads=[bk], writes=[("pt", pi)])
                    minfo[i] = pi

                def emit_mPV(i):
                    hd, mc = msteps[i]
                    pi = minfo[i]
                    ptt = pt[pi]
                    for qs in range(4):
                        bi = qs // 2
                        fst = (mc == 0 and qs % 2 == 0)
                        P.op("pe", lambda e, ptt=ptt, qs=qs, mc=mc, hd=hd, bi=bi, fst=fst: e.matmul(
                            ovs[bi][:, qs % 2, 0:129], lhsT=ptt[:, qs * 128:(qs + 1) * 128], rhs=memV[:, mc, hd, :],
                            start=fst, stop=(mc == 1 and qs % 2 == 1), skip_group_check=True),
                            reads=[("pt", pi), "memV"], writes=["pO%d" % bi])
                    if mc == 1:
                        for bi in range(2):
                            P.op("dve", lambda e, bi=bi: e.reciprocal(out=rden2[:, 2 * bi:2 * bi + 2], in_=ovs[bi][:, :, 128]),
                                 reads=["pO%d" % bi], writes=[("rden2", bi)])
                            P.op("dve", lambda e, bi=bi, hd=hd: e.tensor_tensor(
                                out=otok[:, 2 * bi:2 * bi + 2, hd * 128:(hd + 1) * 128], in0=ovs[bi][:, :, 0:128],
                                in1=rden2[:, 2 * bi:2 * bi + 2].unsqueeze(2).to_broadcast([128, 2, 128]), op=ALU.mult),
                                reads=["pO%d" % bi, ("rden2", bi)], writes=["otok"])

                for i in range(len(msteps) + 2):
                    if i < len(msteps):
                        emit_mS(i)
                    if i - 2 >= 0:
                        emit_mPV(i - 2)
            s5.needs_drain = True

            def s6():
                transposes(otok, lambda cc: oT[:, cc, :], lambda cc: ("oT", cc))
                proj_fm(wmo, "wmo", 4, 0, lambda k: oT[:, k, :], lambda k: ("oT", k), range(KC), l, evac_y)

            def s7():
                postnorm_residual(xt[s_], xk, l, 24)
                dma("pool", xdst.rearrange("(k p) t -> p k t", p=128)[:, :, T0:T0 + NT], xt[s_][:], ckeys(xk), [("scr", "xa")], "xs%d" % s_)
            return [s1, s2, s3, s4, s5, s6, s7]

        pend = []
        for j in range(NTILES):
            steps = []
            for h in range(8):
                nkb = 4 * j + 4
                for kb in range(nkb):
                    steps.append((h, kb, nkb))
            info = {}

            def emit_S(i):
                h, kb, nkb = steps[i]
                ks_ = kst[h % 2]
                kk = ("kst", h % 2)
                koff = 0
                if j == 0 and pre0:
                    ks_ = kst[0]
                    kk = ("kst", 0)
                    koff = h * NT
                if kb == 0 and j * 8 + h + 1 < len(seq):
                    kload(j * 8 + h + 1)
                if kb == 0 and j == 0 and h == 0:
                    wloads(0)
                    if NTILES == 1:
                        wloads(1)
                        wloads(2)
                if kb == 0 and j == 1 and h == 1:
                    wloads(1)
                    wloads(2)
                if kb == 0 and h == 3 and j >= 1 and j + 1 < NTILES and qslot(j + 1) != qslot(j):
                    qload(j + 1)
                dq = kb - 4 * j
                n0 = max(0, dq) * 128
                diag = dq >= 0
                bank, bk = nextB()
                qt_ = qts[qslot(j)]
                P.op("pe", lambda e, bank=bank, ks_=ks_, kb=kb, n0=n0, h=h, diag=diag, qt_=qt_, koff=koff: e.matmul(
                    bank[:, n0:NT], lhsT=ks_[0:70, koff + kb * 128:koff + (kb + 1) * 128], rhs=qt_[0:70, h, n0:NT],
                    start=True, stop=(not diag)), reads=[kk, ("qtile", qslot(j))], writes=[bk])
                if diag:
                    P.op("pe", lambda e, bank=bank, n0=n0: e.matmul(
                        bank[:, n0:n0 + 128], lhsT=identb[:], rhs=maskb[:], start=False, stop=True),
                        reads=["identb", "maskb"], writes=[bk])
                pi = ptc["i"] % 4
                ptc["i"] += 1
                ptt = pt[pi]
                P.op("act", lambda e, bank=bank, ptt=ptt, n0=n0: e.activation(out=ptt[:, n0:NT], in_=bank[:, n0:NT], func=AF.Exp),
                     reads=[bk], writes=[("pt", pi)])
                info[i] = (pi, n0)

            def emit_PV(i):
                h, kb, nkb = steps[i]
                pi, n0 = info[i]
                ptt = pt[pi]
                ob = pO[h % 2]
                okey = "pO%d" % (h % 2)
                ov = ob[:, 0:260].rearrange("p (q c) -> p q c", q=4)
                for qs in range(n0 // 128, 4):
                    first = (kb == 0 and qs == 0)
                    lastmm = (kb == nkb - 1) and (qs == 3)
                    P.op("pe", lambda e, ptt=ptt, qs=qs, kb=kb, h=h, first=first, lastmm=lastmm, ov=ov: e.matmul(
                        ov[:, qs, 0:65], lhsT=ptt[:, qs * 128:(qs + 1) * 128], rhs=Vaug[:, kb, h, :],
                        start=first, stop=lastmm, skip_group_check=True), reads=[("pt", pi), "Vaug"], writes=[okey])
                if kb == nkb - 1:
                    P.op("dve", lambda e, ov=ov: e.reciprocal(out=rden[:], in_=ov[:, :, 64]), reads=[okey], writes=["rden"])
                    P.op("dve", lambda e, ov=ov, h=h: e.tensor_tensor(
                        out=att[:, :, h * 64:(h + 1) * 64], in0=ov[:, :, 0:64], in1=rden[:].unsqueeze(2).to_broadcast([128, 4, 64]),
                        op=ALU.mult), reads=[okey, "rden"], writes=["att"])

            npv = 0
            for i in range(len(steps)):
                h, kb, nkb = steps[i]
                if kb == 0 and h >= 1 and pend:
                    st = pend.pop(0)
                    if getattr(st, "needs_drain", False):
                        while npv < i:
                            emit_PV(npv)
                            npv += 1
                    st()
                emit_S(i)
                while npv <= i - LOOK:
                    emit_PV(npv)
                    npv += 1
            while npv < len(steps):
                emit_PV(npv)
                npv += 1
            while pend:
                pend.pop(0)()
            if j + 1 < NTILES and (j == 0 or qslot(j + 1) == qslot(j)):
                qload(j + 1)
            transposes(att, lambda cc: catT[:, 4 + cc, :], lambda cc: ("catT", cc))
            pend = make_tail(j)
        while pend:
            pend.pop(0)()

        cur["offload"] = False
        P.barrier()
        sb.release(mark_c)
        P.canon = {}
        NT2 = 256
        N2 = S // NT2
        wup = sb.alloc("wup", [128, KC, DFF], BF16)
        wdn = sb.alloc("wdn", [128, 32, D], BF16)
        x2 = [sb.alloc("x2_%d" % i, [128, KC, NT2], F32) for i in range(3)]
        h2 = sb.alloc("h2", [128, KC, NT2], BF16)
        ysq2 = sb.alloc("ysq2", [128, KC, NT2], BF16)
        sqc = [sb.alloc("sqc%d" % i, [128, NT2], BF16) for i in range(8)]
        yr2 = sb.alloc("yr2", [128, KC, NT2], F32)
        rsa = sb.alloc("rsa", [128, NT2], F32)
        rsb = sb.alloc("rsb", [128, NT2], F32)
        rT = sb.alloc("rT", [128, 32, NT2], BF16)
        rtmp = [sb.alloc("rtmp%d" % i, [128, NT2], BF16) for i in range(4)]
        x2src = xa
        x2dst = yT if last else xb

        def x2load(jj):
            dma("sp", x2[jj % 3][:], x2src.rearrange("(k p) t -> p k t", p=128)[:, :, jj * NT2:(jj + 1) * NT2],
                [("scr", "xa")], [("x2", jj % 3)], "xl%d" % (jj % 3))

        def pre2(jj):
            xtile = x2[jj % 3]
            xkey = ("x2", jj % 3)
            for kc in range(KC):
                P.op("act", lambda e, kc=kc: e.activation(out=sqc[kc % 8][:], in_=xtile[:, kc, :], func=AF.Square),
                     reads=[xkey], writes=[("sqc", kc % 8)])
                P.op("pe", lambda e, kc=kc: e.matmul(pA[:, 0:NT2], lhsT=onesD[:], rhs=sqc[kc % 8][:],
                                                     start=(kc == 0), stop=(kc == KC - 1)),
                     reads=[("sqc", kc % 8), "onesD"], writes=["pA"])
            P.op("act", lambda e: e.activation(out=rsa[:], in_=pA[:, 0:NT2], func=AF.Ln, bias=epsc[:, 0:1]),
                 reads=["pA", "epsc"], writes=["rsa"])
            P.op("act", lambda e: e.activation(out=rsa[:], in_=rsa[:], func=AF.Exp, scale=-0.5), reads=["rsa"], writes=["rsa"])
            for kc in range(KC):
                P.op("dve", lambda e, kc=kc: e.scalar_tensor_tensor(
                    out=h2[:, kc, :], in0=xtile[:, kc, :], scalar=gcol(l, 40, kc), in1=rsa[:],
                    op0=ALU.mult, op1=ALU.mult), reads=[xkey, "rsa", "vec"], writes=[("h2", kc)])

        pend2 = []

        def up2(jj):
            for fc in range(32):
                if fc >= 4 and fc % 2 == 0 and pend2:
                    pend2.pop(0)()
                bank, bkey = nextB()
                for k in range(KC):
                    P.op("pe", lambda e, k=k, fc=fc, bank=bank: e.matmul(
                        bank[:, 0:NT2], lhsT=wup[:, k, fc * 128:(fc + 1) * 128], rhs=h2[:, k, :],
                        start=(k == 0), stop=(k == KC - 1)), reads=[("wup", fc // 8), ("h2", k)], writes=[bkey])
                rt = rtmp[fc % 4]
                P.op("act", lambda e, rt=rt, bank=bank: e.activation(out=rt[:], in_=bank[:, 0:NT2], func=AF.Relu),
                     reads=[bkey], writes=[("rtmp", fc % 4)])
                eng = "pool" if (fc % 4 == 1 and fc < 26) else "dve"
                P.op(eng, lambda e, rt=rt, fc=fc: e.tensor_tensor(out=rT[:, fc, :], in0=rt[:], in1=rt[:], op=ALU.mult),
                     reads=[("rtmp", fc % 4)], writes=[("rT", fc)])

        def down2(jj):
            for oc in range(KC):
                bank, bkey = nextB()
                for k in range(32):
                    P.op("pe", lambda e, k=k, oc=oc, bank=bank: e.matmul(
                        bank[:, 0:NT2], lhsT=wdn[:, k, oc * 128:(oc + 1) * 128], rhs=rT[:, k, :],
                        start=(k == 0), stop=(k == 31)), reads=[("wdn", oc // 2), ("rT", k)], writes=[bkey])
                P.op("act", lambda e, oc=oc, bank=bank: e.activation(out=yr2[:, oc, :], in_=bank[:, 0:NT2], func=AF.Copy),
                     reads=[bkey], writes=[("yr2", oc)])
                P.op("act", lambda e, oc=oc, bank=bank: e.activation(out=ysq2[:, oc, :], in_=bank[:, 0:NT2], func=AF.Square),
                     reads=[bkey], writes=[("ysq2", oc)])

        def post2_parts(jj):
            xtile = x2[jj % 3]
            xkey = ("x2", jj % 3)

            def head():
                for kc in range(KC):
                    P.op("pe", lambda e, kc=kc: e.matmul(pO[0][:, 0:NT2], lhsT=onesD[:], rhs=ysq2[:, kc, :],
                                                         start=(kc == 0), stop=(kc == KC - 1)),
                         reads=[("ysq2", kc), "onesD"], writes=["pO0"])
                P.op("act", lambda e: e.activation(out=rsb[:], in_=pO[0][:, 0:NT2], func=AF.Ln, bias=epsc[:, 0:1]),
                     reads=["pO0", "epsc"], writes=["rsb"])
                P.op("act", lambda e: e.activation(out=rsb[:], in_=rsb[:], func=AF.Exp, scale=-0.5), reads=["rsb"], writes=["rsb"])

            def pair(kc):
                def f():
                    P.op("dve", lambda e: e.scalar_tensor_tensor(
                        out=yr2[:, kc, :], in0=yr2[:, kc, :], scalar=gcol(l, 48, kc), in1=rsb[:],
                        op0=ALU.mult, op1=ALU.mult), reads=[("yr2", kc), "rsb", "vec"], writes=[("yr2", kc)])
                    P.op("pool", lambda e: e.tensor_tensor(out=xtile[:, kc, :], in0=xtile[:, kc, :], in1=yr2[:, kc, :],
                                                           op=ALU.add), reads=[xkey, ("yr2", kc)], writes=[xkey])
                return f

            def tail():
                dma("pool", x2dst.rearrange("(k p) t -> p k t", p=128)[:, :, jj * NT2:(jj + 1) * NT2], xtile[:], [xkey],
                    [("scr", "xb")], "xs%d" % (jj % 3))
                if jj + 3 < N2:
                    x2load(jj + 3)
            return [head] + [pair(kc) for kc in range(KC)] + [tail]

        x2load(0)
        load_w(wup, w_up[l], KC, DFF, "wup", after=[("x2", 0)], nkeys=2)
        load_w(wdn, w_down[l], 32, D, "wdn", step=256, after=[("x2", 0)], nkeys=2)
        pre2(0)
        if N2 > 1:
            x2load(1)
        if N2 > 2:
            x2load(2)
        for j in range(N2):
            up2(j)
            if j + 1 < N2:
                pre2(j + 1)
            down2(j)
            pend2.extend(post2_parts(j))
        while pend2:
            pend2.pop(0)()

    xsrc = xT
    for l in range(L):
        layer(l, xsrc)
        xsrc = xb

    P.barrier()

    sems = {e: nc.alloc_semaphore("s_" + e) for e in ENGINES}
    dsems = {k: nc.alloc_semaphore("d_" + k) for k in sorted(dma_keys)}
    run = P.emit(sems, dsems)
    with nc.Block() as block:
        @block.sync
        def _(e):
            run("sp", e)

        @block.scalar
        def _(e):
            run("act", e)

        @block.vector
        def _(e):
            run("dve", e)

        @block.gpsimd
        def _(e):
            run("pool", e)

        @block.tensor
        def _(e):
            run("pe", e)
    return nc, P, sb


def host_consts():
    c = np.zeros((128, 384), np.float32)
    c[:, 0:128] = np.eye(128, dtype=np.float32)
    s_idx = np.arange(128)[:, None]
    t_idx = np.arange(128)[None, :]
    c[:, 128:256] = np.where(s_idx > t_idx, -30000.0, 0.0).astype(np.float32)
    c[:, 256:384] = np.where(s_idx <= t_idx, 1.0, 0.0).astype(np.float32)
    return c


def pack_vecs(inp, L):
    v = np.zeros((128, L, NV), np.float32)

    def col8(a):
        return np.asarray(a, np.float32).reshape(8, 128).T

    def col4(a):
        return np.asarray(a, np.float32).reshape(4, 128).T

    for l in range(L):
        for i, nm in enumerate(["norm_mix_pre", "norm_mix_post", "norm_mem_pre", "norm_mem_post", "norm_memkv",
                                "norm_mlp_pre", "norm_mlp_post"]):
            v[:, l, 8 * i:8 * i + 8] = col8(inp[nm][l])
        v[:, l, 56:60] = col4(inp["conv_b"][l])
        v[:, l, 60:64] = col4(inp["conv_ln_g"][l])
        v[:, l, 64:68] = col4(inp["conv_ln_b"][l])
        v[0:8, l, 68] = np.asarray(inp["b_forget"][l], np.float32)
        v[:, l, NVB:NVB + 8] = np.asarray(inp["b_forget"][l], np.float32)[None, :]
        cw = np.asarray(inp["conv_w"][l], np.float32)
        v[:, l, 69:69 + 124] = cw.T.reshape(4, 128, 31).transpose(1, 0, 2).reshape(128, 124)
    return v


_CACHE = {}


def run_device(inp, S, L, B, dbg=False):
    key = (S, L, dbg)
    if key not in _CACHE:
        _CACHE[key] = build(S, L, dbg)
    nc = _CACHE[key][0]
    shared = {
        "w_in": np.ascontiguousarray(inp["w_in"][:L], np.float32),
        "w_out": np.ascontiguousarray(inp["w_out"][:L], np.float32),
        "w_mq": np.ascontiguousarray(inp["w_mq"][:L], np.float32),
        "w_mk": np.ascontiguousarray(inp["w_mk"][:L], np.float32),
        "w_mv": np.ascontiguousarray(inp["w_mv"][:L], np.float32),
        "w_mo": np.ascontiguousarray(inp["w_mo"][:L], np.float32),
        "w_up": np.ascontiguousarray(inp["w_up"][:L], np.float32),
        "w_down": np.ascontiguousarray(inp["w_down"][:L], np.float32),
        "vecs": pack_vecs(inp, L),
        "cst": host_consts(),
    }
    x = np.asarray(inp["x"], np.float32)
    mem = np.asarray(inp["mem"], np.float32)
    in_maps = []
    for b in range(B):
        m = dict(shared)
        m["xT"] = np.ascontiguousarray(x[b].T)
        m["memT"] = np.ascontiguousarray(mem[b].T)
        in_maps.append(m)
    res = run_bass_kernel_spmd(nc, in_maps, core_ids=list(range(B)))
    return res


def kernel(**inputs):
    B, S, _ = inputs["x"].shape
    L = inputs["w_in"].shape[0]
    res = run_device(inputs, S, L, B)
    out = np.empty((B, S, D), np.float32)
    for b in range(B):
        out[b] = res.results[b]["yT"].T
    return out
```

```python
import numpy as np
import ml_dtypes
import concourse.bass as bass
import concourse.mybir as mybir
from concourse.bass_utils import run_bass_kernel_spmd

F32 = mybir.dt.float32
BF16 = mybir.dt.bfloat16
AF = mybir.ActivationFunctionType
ALU = mybir.AluOpType

D = 1024
KC = 8
NT = 512
NMEM = 256
IN_COLS = 2568
DFF = 4096
EPS = 1e-6
ENGINES = ("pe", "act", "dve", "pool", "sp")
NVB = 69 + 124
NV = NVB + 8


class Prog:
    def __init__(self):
        self.ops = []
        self.last_writer = {}
        self.readers = {}
        self.dma_last = {}
        self.last_on_eng = {}
        self.canon = {}

    def op(self, eng, fn, reads=(), writes=(), dma_key=None, extra_deps=()):
        i = len(self.ops)
        reads = [self.canon.get(r, r) for r in reads]
        writes = [self.canon.get(w, w) for w in writes]
        deps = {}
        for r in reads:
            j = self.last_writer.get(r)
            if j is not None:
                deps[j] = True
        for w in writes:
            j = self.last_writer.get(w)
            if j is not None:
                deps.setdefault(j, False)
            seen = set()
            for j in reversed(self.readers.get(w, ())):
                pj = self.ops[j]
                if pj["dma_key"] is None and pj["eng"] != "pool":
                    if pj["eng"] in seen:
                        continue
                    seen.add(pj["eng"])
                deps.setdefault(j, False)
        for j in extra_deps:
            deps[j] = True
        if dma_key is not None:
            j = self.dma_last.get(dma_key)
            if j is not None:
                deps[j] = True
            self.dma_last[dma_key] = i
        deps.pop(i, None)
        for w in writes:
            self.last_writer[w] = i
            self.readers[w] = []
        for r in reads:
            self.readers.setdefault(r, []).append(i)
        self.ops.append(dict(eng=eng, fn=fn, deps=deps, dma_key=dma_key, signal=False))
        if fn is not None:
            self.last_on_eng[eng] = i
        return i

    def barrier(self):
        alld = list(self.last_on_eng.values()) + list(self.dma_last.values())
        for e in ENGINES:
            self.op(e, None, extra_deps=alld)

    def emit(self, sems, dma_sems):
        ops = self.ops
        for i, o in enumerate(ops):
            need = []
            for j, raw in o["deps"].items():
                pj = ops[j]
                if pj["fn"] is None:
                    continue
                if pj["dma_key"] is None and pj["eng"] == o["eng"]:
                    if o["eng"] == "pe":
                        continue
                need.append(j)
                if pj["dma_key"] is None:
                    pj["signal"] = True
            o["need"] = need
        cnt = {e: 0 for e in ENGINES}
        dcnt = {}
        for o in ops:
            if o["fn"] is None:
                continue
            if o["dma_key"] is not None:
                k = o["dma_key"]
                dcnt[k] = dcnt.get(k, 0) + 16
                o["sig"] = (dma_sems[k], dcnt[k])
            elif o["signal"]:
                cnt[o["eng"]] += 1
                o["sig"] = (sems[o["eng"]], cnt[o["eng"]])
        per_eng = {e: [] for e in ENGINES}
        for o in ops:
            per_eng[o["eng"]].append(o)
        self.counts = dict(cnt)
        self.n_instr = {e: len(per_eng[e]) for e in ENGINES}

        def run(eng_name, eng):
            waited = {}
            for o in per_eng[eng_name]:
                req = {}
                for j in o["need"]:
                    s, v = ops[j]["sig"]
                    key = id(s)
                    if v > req.get(key, (None, 0))[1]:
                        req[key] = (s, v)
                for key, (s, v) in req.items():
                    if waited.get(key, 0) >= v:
                        continue
                    eng.wait_ge(s, v)
                    waited[key] = v
                if o["fn"] is None:
                    continue
                ins = o["fn"](eng)
                if o["dma_key"] is not None:
                    ins.then_inc(o["sig"][0], 16)
                elif o["signal"]:
                    ins.then_inc(o["sig"][0], 1)

        return run


class SB:
    def __init__(self, nc, base=16512, limit=229376 - 64):
        self.nc = nc
        self.off = base
        self.limit = limit
        self.n = 0
        self.peak = base

    def alloc(self, name, shape, dt):
        esz = 2 if dt == BF16 else 4
        nbytes = esz
        for s in shape[1:]:
            nbytes *= s
        self.off = (self.off + 63) // 64 * 64
        assert self.off + nbytes <= self.limit, (name, self.off, nbytes, self.limit)
        self.n += 1
        t = self.nc.alloc_sbuf_tensor_at("%s_%d" % (name, self.n), list(shape), dt, offset=self.off)
        self.offs = getattr(self, "offs", {})
        self.offs[name] = self.off
        self.off += nbytes
        self.peak = max(self.peak, self.off)
        return t

    def mark(self):
        return self.off

    def release(self, m):
        self.off = m


def build(S, L, dbg=False):
    NTILES = S // NT
    NBLK = S // 128
    nc = bass.Bass("TRN2", target_bir_lowering=False)
    P = Prog()

    def din(name, shape, dt=F32):
        return nc.dram_tensor(name, list(shape), dt, kind="ExternalInput").ap()

    xT = din("xT", [D, S])
    memT = din("memT", [D, NMEM])
    w_in = din("w_in", [L, D, IN_COLS])
    w_out = din("w_out", [L, D, D])
    w_mq = din("w_mq", [L, D, 512])
    w_mk = din("w_mk", [L, D, 512])
    w_mv = din("w_mv", [L, D, 512])
    w_mo = din("w_mo", [L, 512, D])
    w_up = din("w_up", [L, D, DFF])
    w_down = din("w_down", [L, DFF, D])
    vecs = din("vecs", [128, L, NV])
    cst = din("cst", [128, 384])
    yT = nc.dram_tensor("yT", [D, S], F32, kind="ExternalOutput").ap()
    kind_dbg = "ExternalOutput" if dbg else "Internal"
    xa = nc.dram_tensor("xa", [D, S], F32, kind=kind_dbg).ap()
    xb = nc.dram_tensor("xb", [D, S], F32, kind=kind_dbg).ap()
    Qm = nc.dram_tensor("Qm", [512, S], BF16, kind=kind_dbg).ap()
    Km = nc.dram_tensor("Km", [512, S], BF16, kind=kind_dbg).ap()
    Qd = nc.dram_tensor("Qd", [8, 6, S], BF16, kind=kind_dbg).ap()
    Kd = nc.dram_tensor("Kd", [8, 6, S], BF16, kind=kind_dbg).ap()
    uTd = nc.dram_tensor("uTd", [512, S], BF16, kind=kind_dbg).ap()

    sb = SB(nc)
    vec = sb.alloc("vec", [128, L, NV], F32)
    cstf = sb.alloc("cstf", [128, 384], F32)
    identb = sb.alloc("identb", [128, 128], BF16)
    maskb = sb.alloc("maskb", [128, 128], BF16)
    onesD = sb.alloc("onesD", [128, 128], BF16)
    onesC = sb.alloc("onesC", [128, 128], BF16)
    negb = sb.alloc("negb", [8, L], F32)
    epsc = sb.alloc("epsc", [128, 1], F32)
    onec = sb.alloc("onec", [128, 1], F32)
    mark_c = sb.mark()
    xt0 = sb.alloc("xt0", [128, KC, NT], F32)
    sq = sb.alloc("sq", [128, KC, NT], BF16)
    rstd = sb.alloc("rstd", [128, NT], F32)
    yraw = sb.alloc("yraw", [128, KC, NT], F32)
    mark_p2 = sb.mark()
    hT1 = sb.alloc("hT", [128, KC, NT], BF16)
    xt = [xt0, sb.alloc("xt1", [128, KC, NT], F32)]
    Vaug = sb.alloc("Vaug", [128, NBLK, 8, 65], BF16)
    base_mark = sb.mark()
    cur = {"hT": hT1, "offload": False}

    pA = nc.alloc_psum_tensor("pA", [128, NT], F32)
    pB = [nc.alloc_psum_tensor("pB%d" % i, [128, NT], F32) for i in range(4)]
    pO = [nc.alloc_psum_tensor("pO%d" % i, [128, NT], F32) for i in range(2)]
    pT = nc.alloc_psum_tensor("pT", [128, 2 * NT], BF16)
    bstate = {"i": 0}

    def nextB():
        i = bstate["i"] % 4
        bstate["i"] += 1
        return pB[i], ("pB", i)

    dma_keys = set()
    wq = {"i": 0}

    def dma(eng, out, in_, reads, writes, key):
        dma_keys.add(key)
        return P.op(eng, lambda e: e.dma_start(out=out, in_=in_), reads=reads, writes=writes, dma_key=key)

    def load_w(dst, src, K, C, wkey, step=1024, kstep=None, after=(), nkeys=4):
        srcv = src.rearrange("(k p) c -> p k c", p=128)
        if kstep is not None:
            for k0 in range(0, K, kstep):
                key = "wl%d" % (wq["i"] % nkeys)
                wq["i"] += 1
                dma("pool", dst[:, k0:k0 + kstep, :], srcv[:, k0:k0 + kstep, :], list(after), [(wkey, k0 // kstep)], key)
            return
        for c0 in range(0, C, step):
            c1 = min(C, c0 + step)
            key = "wl%d" % (wq["i"] % nkeys)
            wq["i"] += 1
            dma("pool", dst[:, :, c0:c1], srcv[:, :, c0:c1], list(after), [(wkey, c0 // step)], key)

    dma("sp", vec[:], vecs, [], ["vec"], "ld_misc")
    dma("sp", cstf[:], cst, [], ["cstf"], "ld_misc2")
    P.op("dve", lambda e: e.tensor_copy(out=identb[:], in_=cstf[:, 0:128]), reads=["cstf"], writes=["identb"])
    P.op("dve", lambda e: e.tensor_copy(out=maskb[:], in_=cstf[:, 128:256]), reads=["cstf"], writes=["maskb"])
    P.op("dve", lambda e: e.memset(onesD[:], 1.0 / D), writes=["onesD"])
    P.op("dve", lambda e: e.memset(onesC[:], 1.0 / 512), writes=["onesC"])
    P.op("dve", lambda e: e.memset(epsc[:], EPS), writes=["epsc"])
    P.op("dve", lambda e: e.memset(onec[:], 1.0), writes=["onec"])
    P.op("dve", lambda e: e.tensor_scalar(out=negb[:], in0=vec[0:8, :, 68], scalar1=-1.0, scalar2=None, op0=ALU.mult),
         reads=["vec"], writes=["negb"])

    ones3 = sb.alloc("ones3", [8, 3, S], BF16)
    sb.release(base_mark)
    P.op("pool", lambda e: e.memset(ones3[:], 1.0), writes=["ones3"])
    dma("sp", Qd[:, 3:6, :], ones3[:], ["ones3"], [("scr", "qd1")], "dst0")
    dma("sp", Kd[:, 0:3, :], ones3[:], ["ones3"], [("scr", "kd1")], "dst1")

    def gcol(l, base, k):
        return vec[:, l, base + k:base + k + 1]

    def prenorm(xtile, xkey, l, gbase, ncols=NT):
        for kc in range(KC):
            if cur["offload"]:
                P.op("pool", lambda e, kc=kc: e.tensor_tensor(out=sq[:, kc, 0:ncols], in0=xtile[:, kc, 0:ncols],
                                                              in1=xtile[:, kc, 0:ncols], op=ALU.mult),
                     reads=[(xkey, kc)], writes=[("sq", kc)])
            else:
                P.op("act", lambda e, kc=kc: e.activation(out=sq[:, kc, 0:ncols], in_=xtile[:, kc, 0:ncols], func=AF.Square),
                     reads=[(xkey, kc)], writes=[("sq", kc)])
            P.op("pe", lambda e, kc=kc: e.matmul(pA[:, 0:ncols], lhsT=onesD[:], rhs=sq[:, kc, 0:ncols],
                                                 start=(kc == 0), stop=(kc == KC - 1)),
                 reads=[("sq", kc), "onesD"], writes=["pA"])
        P.op("act", lambda e: e.activation(out=rstd[:, 0:ncols], in_=pA[:, 0:ncols], func=AF.Ln, bias=epsc[:, 0:1]),
             reads=["pA", "epsc"], writes=["rstd"])
        P.op("act", lambda e: e.activation(out=rstd[:, 0:ncols], in_=rstd[:, 0:ncols], func=AF.Exp, scale=-0.5),
             reads=["rstd"], writes=["rstd"])
        hT = cur["hT"]
        for kc in range(KC):
            P.op("dve", lambda e, kc=kc: e.scalar_tensor_tensor(
                out=hT[:, kc, 0:ncols], in0=xtile[:, kc, 0:ncols], scalar=gcol(l, gbase, kc), in1=rstd[:, 0:ncols],
                op0=ALU.mult, op1=ALU.mult), reads=[(xkey, kc), "rstd", "vec"], writes=[("hT", kc)])

    def postnorm_residual(xtile, xkey, l, gbase):
        for kc in range(KC):
            P.op("pe", lambda e, kc=kc: e.matmul(pA[:, :], lhsT=onesD[:], rhs=sq[:, kc, :],
                                                 start=(kc == 0), stop=(kc == KC - 1)),
                 reads=[("sq", kc), "onesD"], writes=["pA"])
        P.op("act", lambda e: e.activation(out=rstd[:], in_=pA[:], func=AF.Ln, bias=epsc[:, 0:1]),
             reads=["pA", "epsc"], writes=["rstd"])
        P.op("act", lambda e: e.activation(out=rstd[:], in_=rstd[:], func=AF.Exp, scale=-0.5),
             reads=["rstd"], writes=["rstd"])
        for kc in range(KC):
            P.op("dve", lambda e, kc=kc: e.scalar_tensor_tensor(
                out=yraw[:, kc, :], in0=yraw[:, kc, :], scalar=gcol(l, gbase, kc), in1=rstd[:],
                op0=ALU.mult, op1=ALU.mult), reads=[("yraw", kc), "rstd", "vec"], writes=[("yraw", kc)])
            P.op("pool" if kc < 5 else "dve", lambda e, kc=kc: e.tensor_tensor(
                out=xtile[:, kc, :], in0=xtile[:, kc, :], in1=yraw[:, kc, :], op=ALU.add),
                reads=[(xkey, kc), ("yraw", kc)], writes=[(xkey, kc)])

    def ckeys(xkey):
        return [(xkey, k_) for k_ in range(KC)]

    def proj_fm(wt, wkey, nk, col0, rhs_of, rkeys, oc_list, l, evac):
        for oc in oc_list:
            bank, bkey = nextB()
            for k in range(nk):
                P.op("pe", lambda e, k=k, oc=oc, bank=bank: e.matmul(
                    bank[:, :], lhsT=wt[:, k, col0 + oc * 128:col0 + (oc + 1) * 128], rhs=rhs_of(k),
                    start=(k == 0), stop=(k == nk - 1)), reads=[wkey(k, oc) if callable(wkey) else (wkey, 0), rkeys(k)], writes=[bkey])
            evac(oc, bank, bkey)

    def evac_y(oc, bank, bkey):
        if cur["offload"]:
            P.op("dve", lambda e: e.tensor_copy(out=yraw[:, oc, :], in_=bank[:, :]), reads=[bkey], writes=[("yraw", oc)])
            P.op("pool", lambda e: e.tensor_tensor(out=sq[:, oc, :], in0=yraw[:, oc, :], in1=yraw[:, oc, :], op=ALU.mult),
                 reads=[("yraw", oc)], writes=[("sq", oc)])
            return
        P.op("act", lambda e: e.activation(out=yraw[:, oc, :], in_=bank[:, :], func=AF.Copy),
             reads=[bkey], writes=[("yraw", oc)])
        P.op("act", lambda e: e.activation(out=sq[:, oc, :], in_=bank[:, :], func=AF.Square),
             reads=[bkey], writes=[("sq", oc)])

    def layer(l, xsrc):
        last = (l == L - 1)
        P.barrier()
        sb.release(base_mark)
        P.canon = {}
        cur["hT"] = hT1
        hT = hT1
        P.op("dve", lambda e: e.memset(Vaug[:], 1.0), writes=["Vaug"])
        win = sb.alloc("win", [128, KC, IN_COLS], BF16)
        u = sb.alloc("u", [128, 4, NT + 30], BF16)
        halo = sb.alloc("halo", [128, 4, 30], BF16)
        dg = sb.alloc("dg", [128, 124, 128], BF16)
        acc = yraw
        sg1 = sb.alloc("sg", [128, NT], F32)
        sg = [sg1, sg1]
        qk = [sb.alloc("qk%d" % i, [128, NT], BF16) for i in range(2)]
        uo = sb.alloc("uo", [128, 4, NT], BF16)
        cbf = uo
        csq = nc.alloc_sbuf_tensor_at("csq_%d" % l, [128, 4, NT], BF16, offset=sb.offs["yraw"] + 4 * NT * 4)
        rl = sb.alloc("rl", [128, NT], F32)
        nmr = sb.alloc("nmr", [128, NT], F32)
        m2 = nmr
        ft = sb.alloc("ft", [128, 4, 8], F32)
        cs = [sb.alloc("cs%d" % i, [8, NT], F32) for i in range(2)]
        r1 = sb.alloc("r1", [8, NT], F32)
        DQ = sb.alloc("DQ", [8, 3, NT], BF16)
        DK = sb.alloc("DK", [8, 3, NT], BF16)
        load_w(win, w_in[l], KC, IN_COLS, "win", nkeys=1)
        P.op("pool", lambda e: e.memset(u[:, :, 0:30], 0.0), writes=["u_halo"])
        CW = 69
        pending = []
        pendingB = []

        def build_dg():
            for idx in range(124):
                P.op("dve", lambda e, idx=idx: e.tensor_scalar(
                    out=dg[:, idx, :], in0=identb[:], scalar1=vec[:, l, CW + idx:CW + idx + 1], scalar2=None, op0=ALU.mult),
                    reads=["identb", "vec"], writes=[("dg", idx)])
        def xload(jj, src):
            dma("sp", xt[jj % 2][:], src.rearrange("(k p) t -> p k t", p=128)[:, :, jj * NT:(jj + 1) * NT], [],
                ckeys(("xt", jj % 2)), "xl%d" % (jj % 2))
        xload(0, xsrc)
        for j in range(NTILES):
            T0 = j * NT
            s = j % 2
            xk = ("xt", s)
            if j + 1 < NTILES:
                xload(j + 1, xsrc)
            if j == 0:
                prenorm(xt[s], xk, l, 0)
            hkeys = lambda k: ("hT", k)
            hrhs = lambda k: hT[:, k, :]
            for cc in range(4):
                bank_a, ka = nextB()
                for k in range(KC):
                    P.op("pe", lambda e, k=k, cc=cc, bank=bank_a: e.matmul(
                        bank[:, :], lhsT=win[:, k, cc * 128:(cc + 1) * 128], rhs=hT[:, k, :],
                        start=(k == 0), stop=(k == KC - 1)), reads=[("win", 0), ("hT", k)], writes=[ka])
                bank_g, kg = nextB()
                for k in range(KC):
                    P.op("pe", lambda e, k=k, cc=cc, bank=bank_g: e.matmul(
                        bank[:, :], lhsT=win[:, k, 512 + cc * 128:512 + (cc + 1) * 128], rhs=hT[:, k, :],
                        start=(k == 0), stop=(k == KC - 1)), reads=[("win", 0), ("hT", k)], writes=[kg])
                sgt = sg[cc % 2]
                P.op("act", lambda e, bank=bank_g, sgt=sgt: e.activation(out=sgt[:], in_=bank[:, :], func=AF.Sigmoid),
                     reads=[kg], writes=["sg"])
                P.op("dve", lambda e, bank=bank_a, sgt=sgt, cc=cc: e.tensor_tensor(
                    out=u[:, cc, 30:30 + NT], in0=bank[:, :], in1=sgt[:], op=ALU.mult),
                    reads=[ka, "sg"], writes=[("u", cc)])
            for which, col0, dst in (("q", 1024, Qm), ("k", 1536, Km)):
                for c in range(4):
                    bank, bk = nextB()
                    for k in range(KC):
                        P.op("pe", lambda e, k=k, c=c, bank=bank, col0=col0: e.matmul(
                            bank[:, :], lhsT=win[:, k, col0 + c * 128:col0 + (c + 1) * 128], rhs=hT[:, k, :],
                            start=(k == 0), stop=(k == KC - 1)), reads=[("win", 1), ("hT", k)], writes=[bk])
                    qs_ = qk[c % 2]
                    sc = 0.125 if which == "q" else 1.0
                    P.op("act", lambda e, bank=bank, qs_=qs_, sc=sc: e.activation(out=qs_[:], in_=bank[:, :], func=AF.Copy, scale=sc),
                         reads=[bk], writes=[("qk", c % 2)])
                    dma("sp", dst[c * 128:(c + 1) * 128, T0:T0 + NT], qs_[:], [("qk", c % 2)], [("scr", which)], "qkst%d" % (c % 2))
            if pending:
                pending.pop(0)()
            for sbk in range(4):
                bank, bk = nextB()
                for k in range(KC):
                    P.op("pe", lambda e, k=k, sbk=sbk, bank=bank: e.matmul(
                        bank[:, :], lhsT=hT[:, k, sbk * 128:(sbk + 1) * 128], rhs=win[:, k, 2048:2560],
                        start=(k == 0), stop=(k == KC - 1)), reads=[("win", 2), ("hT", k)], writes=[bk])
                blk = 4 * j + sbk
                P.op("act", lambda e, bank=bank, blk=blk: e.activation(
                    out=Vaug[:, blk, :, 0:64], in_=bank[:, :].rearrange("p (h d) -> p h d", h=8), func=AF.Copy),
                    reads=[bk], writes=["Vaug"])
            Ucum = cstf[:, 256:384]
            for sbk in range(4):
                bank, bk = nextB()
                for k in range(KC):
                    P.op("pe", lambda e, k=k, sbk=sbk, bank=bank: e.matmul(
                        bank[:, 0:8], lhsT=hT[:, k, sbk * 128:(sbk + 1) * 128], rhs=win[:, k, 2560:2568],
                        start=(k == 0), stop=(k == KC - 1)), reads=[("win", 2), ("hT", k)], writes=[bk])
                P.op("dve", lambda e, sbk=sbk, bank=bank: e.tensor_tensor(
                    out=ft[:, sbk, :], in0=bank[:, 0:8], in1=vec[:, l, NVB:NVB + 8], op=ALU.add),
                    reads=[bk, "vec"], writes=["ft"])
            def cs_part(j=j, T0=T0):
                P.op("act", lambda e: e.activation(out=ft[:], in_=ft[:], func=AF.Exp, scale=-1.0), reads=["ft"], writes=["ft"])
                P.op("act", lambda e: e.activation(out=ft[:], in_=ft[:], func=AF.Ln, bias=onec[:, 0:1]), reads=["ft", "onec"], writes=["ft"])
                csn, csp = cs[j % 2], cs[(j + 1) % 2]
                for sbk in range(4):
                    bank, bk = nextB()
                    P.op("pe", lambda e, sbk=sbk, bank=bank: e.matmul(
                        bank[0:8, 0:128], lhsT=ft[:, sbk, :], rhs=Ucum, start=True, stop=True),
                        reads=["ft", "cstf"], writes=[bk])
                    if sbk == 0:
                        carry = 0.0 if j == 0 else csp[:, NT - 1:NT]
                    else:
                        carry = csn[:, sbk * 128 - 1:sbk * 128]
                    P.op("dve", lambda e, sbk=sbk, bank=bank, carry=carry, csn=csn: e.tensor_scalar(
                        out=csn[:, sbk * 128:(sbk + 1) * 128], in0=bank[0:8, 0:128], scalar1=carry, scalar2=None, op0=ALU.add),
                        reads=[bk, ("cs", (j + 1) % 2), ("cs", j % 2)], writes=[("cs", j % 2)])
                ck = ("cs", j % 2)
                P.op("dve", lambda e, csn=csn: e.tensor_copy(out=DK[:, 0, :], in_=csn[:]), reads=[ck], writes=["DK"])
                P.op("dve", lambda e, csn=csn: e.tensor_tensor(out=r1[:], in0=csn[:], in1=DK[:, 0, :], op=ALU.subtract), reads=[ck, "DK"], writes=["r1"])
                P.op("dve", lambda e: e.tensor_copy(out=DK[:, 1, :], in_=r1[:]), reads=["r1"], writes=["DK"])
                P.op("dve", lambda e: e.tensor_tensor(out=r1[:], in0=r1[:], in1=DK[:, 1, :], op=ALU.subtract), reads=["r1", "DK"], writes=["r1"])
                P.op("dve", lambda e: e.tensor_copy(out=DK[:, 2, :], in_=r1[:]), reads=["r1"], writes=["DK"])
                P.op("dve", lambda e: e.tensor_scalar(out=DQ[:], in0=DK[:], scalar1=-1.0, scalar2=None, op0=ALU.mult),
                     reads=["DK"], writes=["DQ"])
                dma("sp", Qd[:, 0:3, T0:T0 + NT], DQ[:], ["DQ"], [("scr", "qd")], "dst0")
                dma("sp", Kd[:, 3:6, T0:T0 + NT], DK[:], ["DK"], [("scr", "kd")], "dst1")
            if j + 1 < NTILES:
                prenorm(xt[(j + 1) % 2], ("xt", (j + 1) % 2), l, 0)
            if pendingB:
                pendingB.pop(0)()
            if j == 0:
                build_dg()
            if j > 0:
                P.op("pool", lambda e: e.tensor_copy(out=u[:, :, 0:30], in_=halo[:]), reads=["halo"], writes=["u_halo"])
            for cc in range(4):
                bank, bk = nextB()
                for k in range(31):
                    P.op("pe", lambda e, cc=cc, k=k, bank=bank: e.matmul(
                        bank[:, :], lhsT=dg[:, cc * 31 + k, :], rhs=u[:, cc, k:k + NT], start=(k == 0), stop=(k == 30)),
                        reads=[("u", cc), "u_halo", ("dg", cc * 31 + k)], writes=[bk])
                P.op("act", lambda e, cc=cc, bank=bank: e.activation(
                    out=acc[:, cc, :], in_=bank[:, :], func=AF.Identity, bias=vec[:, l, 56 + cc:57 + cc]),
                    reads=[bk, "vec"], writes=[("acc", cc)])
            cs_part()
            P.op("pool", lambda e: e.tensor_copy(out=halo[:], in_=u[:, :, NT:NT + 30]),
                 reads=[("u", 0), ("u", 1), ("u", 2), ("u", 3)], writes=["halo"])
            def ln_part(T0=T0):
                acck = [("acc", c) for c in range(4)]
                P.op("act", lambda e: e.activation(out=csq[:], in_=acc[:, 0:4, :], func=AF.Square), reads=acck, writes=["csq"])
                P.op("dve", lambda e: e.tensor_copy(out=cbf[:], in_=acc[:, 0:4, :]), reads=acck, writes=["uo"])
                for cc in range(4):
                    P.op("pe", lambda e, cc=cc: e.matmul(pO[0][:, :], lhsT=onesC[:], rhs=cbf[:, cc, :], start=(cc == 0), stop=(cc == 3)),
                         reads=["uo", "onesC"], writes=["pO0"])
                for cc in range(4):
                    P.op("pe", lambda e, cc=cc: e.matmul(pO[1][:, :], lhsT=onesC[:], rhs=csq[:, cc, :], start=(cc == 0), stop=(cc == 3)),
                         reads=["csq", "onesC"], writes=["pO1"])
                P.op("act", lambda e: e.activation(out=m2[:], in_=pO[0][:, :], func=AF.Square), reads=["pO0"], writes=["nmr"])
                P.op("dve", lambda e: e.tensor_tensor(out=rl[:], in0=pO[1][:, :], in1=m2[:], op=ALU.subtract), reads=["pO1", "nmr"], writes=["rl"])
                P.op("act", lambda e: e.activation(out=rl[:], in_=rl[:], func=AF.Ln, bias=epsc[:, 0:1]), reads=["rl", "epsc"], writes=["rl"])
                P.op("act", lambda e: e.activation(out=rl[:], in_=rl[:], func=AF.Exp, scale=-0.5), reads=["rl"], writes=["rl"])
                P.op("dve", lambda e: e.scalar_tensor_tensor(out=nmr[:], in0=pO[0][:, :], scalar=-1.0, in1=rl[:], op0=ALU.mult, op1=ALU.mult),
                     reads=["pO0", "rl"], writes=["nmr"])

            def ln_part_b(T0=T0):
                for cc in range(4):
                    P.op("dve", lambda e, cc=cc: e.tensor_tensor(out=acc[:, cc, :], in0=acc[:, cc, :], in1=rl[:], op=ALU.mult),
                         reads=[("acc", cc), "rl"], writes=[("acc", cc)])
                    P.op("dve", lambda e, cc=cc: e.tensor_tensor(out=acc[:, cc, :], in0=acc[:, cc, :], in1=nmr[:], op=ALU.add),
                         reads=[("acc", cc), "nmr"], writes=[("acc", cc)])
                    P.op("act", lambda e, cc=cc: e.activation(out=uo[:, cc, :], in_=acc[:, cc, :], func=AF.Silu,
                                                              scale=vec[:, l, 60 + cc:61 + cc], bias=vec[:, l, 64 + cc:65 + cc]),
                         reads=[("acc", cc), "vec"], writes=["uo"])
                dma("sp", uTd.rearrange("(c p) t -> p c t", p=128)[:, :, T0:T0 + NT], uo[:], ["uo"], [("scr", "u")], "ust")
            pending.append(ln_part)
            pendingB.append(ln_part_b)

        while pending:
            pending.pop(0)()
        while pendingB:
            pendingB.pop(0)()

        P.barrier()
        sb.release(base_mark)
        wout = sb.alloc("wout", [128, KC, D], BF16)
        wmq = sb.alloc("wmq", [128, KC, 512], BF16)
        wmo = sb.alloc("wmo", [128, 4, D], BF16)
        memKT = sb.alloc("memKT", [128, 4, NMEM], BF16)
        memV = sb.alloc("memV", [128, 2, 4, 129], BF16)
        kst = [sb.alloc("kst%d" % i, [128, S], BF16) for i in range(2)]
        qtile = sb.alloc("qtile", [128, 8, NT], BF16)
        catT = sb.alloc("catT", [128, 8, NT], BF16)
        att = sb.alloc("att", [128, 4, 512], BF16)
        pt = [sb.alloc("pt%d" % i, [128, NT], BF16) for i in range(4)]
        rden = sb.alloc("rden", [128, 4], F32)
        qm = sb.alloc("qm", [128, 4, NT], BF16)
        m1b = sb.mark()
        wmk = sb.alloc("wmk", [128, KC, 512], BF16)
        wmv = sb.alloc("wmv", [128, KC, 512], BF16)
        def mem_prep():
            mk_ = "yrawm"
            dma("sp", yraw[:, :, 0:NMEM], memT.rearrange("(k p) t -> p k t", p=128), [], ckeys(mk_) + [("yraw", k_) for k_ in range(KC)], "ld_misc")
            prenorm(yraw, mk_, l, 32, ncols=NMEM)
            P.op("dve", lambda e: e.memset(memV[:], 1.0), writes=["memV"])
            for hd in range(4):
                bank, bk = nextB()
                for k in range(KC):
                    P.op("pe", lambda e, k=k, hd=hd, bank=bank: e.matmul(
                        bank[:, 0:NMEM], lhsT=wmk[:, k, hd * 128:(hd + 1) * 128], rhs=hT[:, k, 0:NMEM],
                        start=(k == 0), stop=(k == KC - 1)), reads=[("wmk", 0), ("hT", k)], writes=[bk])
                P.op("act", lambda e, hd=hd, bank=bank: e.activation(out=memKT[:, hd, :], in_=bank[:, 0:NMEM], func=AF.Copy, scale=128.0 ** -0.5),
                     reads=[bk], writes=["memKT"])
            for mc in range(2):
                bank, bk = nextB()
                for k in range(KC):
                    P.op("pe", lambda e, k=k, mc=mc, bank=bank: e.matmul(
                        bank[:, :], lhsT=hT[:, k, mc * 128:(mc + 1) * 128], rhs=wmv[:, k, :],
                        start=(k == 0), stop=(k == KC - 1)), reads=[("wmv", 0), ("hT", k)], writes=[bk])
                P.op("act", lambda e, mc=mc, bank=bank: e.activation(
                    out=memV[:, mc, :, 0:128], in_=bank[:, :].rearrange("p (h d) -> p h d", h=4), func=AF.Copy),
                    reads=[bk], writes=["memV"])

        xdst = xa
        ptc = {"i": 0}
        seq = [(jj, hh) for jj in range(NTILES) for hh in range(8)]

        pre0 = (S >= 8 * NT)

        def kload(idx):
            jj, hh = seq[idx]
            if jj == 0 and pre0:
                return
            kl_ = (jj + 1) * NT
            dma("sp", kst[hh % 2][0:64, 0:kl_], Km[hh * 64:(hh + 1) * 64, 0:kl_], [("scr", "k")], [("kst", hh % 2)], "kl%d" % (hh % 2))
            dma("sp", kst[hh % 2][64:70, 0:kl_], Kd[hh, :, 0:kl_], [("scr", "kd")], [("kst", hh % 2)], "kd%d" % (hh % 2))
        def uload(jj):
            dma("sp", catT[:, 0:4, :], uTd.rearrange("(c p) t -> p c t", p=128)[:, :, jj * NT:(jj + 1) * NT], [("scr", "u")], ["catT_u"], "ul")

        qtile2 = nc.alloc_sbuf_tensor_at("qtile2_%d" % l, [128, 8, NT], BF16, offset=sb.offs["wmv"])
        qts = [qtile, qtile2]

        def qslot(jj):
            return 1 if (jj >= 3 and jj % 2 == 1) else 0

        def qload(jj):
            qs_ = qslot(jj)
            qt_ = qts[qs_]
            wr = [("qtile", qs_)] + ([("wmv", 0)] if qs_ == 1 else [])
            dma("sp", qt_[0:64, :, :], Qm.rearrange("(h r) t -> r h t", r=64)[:, :, jj * NT:(jj + 1) * NT], [("scr", "q")], wr, "ql0")
            dma("sp", qt_[64:70, :, :], Qd.rearrange("h r t -> r h t")[:, :, jj * NT:(jj + 1) * NT], [("scr", "qd")], wr, "ql1")
        qload(0)
        if pre0:
            dma("sp", kst[0][0:64, 0:8 * NT].rearrange("r (h t) -> r h t", h=8),
                Km.rearrange("(h r) t -> r h t", r=64)[:, :, 0:NT], [("scr", "k")], [("kst", 0)], "kl0")
            dma("sp", kst[0][64:70, 0:8 * NT].rearrange("r (h t) -> r h t", h=8),
                Kd.rearrange("h r t -> r h t")[:, :, 0:NT], [("scr", "kd")], [("kst", 0)], "kd0")
        else:
            kload(0)
        uload(0)
        xload(0, xsrc)
        def wloads(stage):
            if stage == 0:
                load_w(wout, w_out[l], KC, D, "wout", after=[("qtile", 0), ("kst", 0), ("kst", 1)])
            elif stage == 1:
                load_w(wmq, w_mq[l], KC, 512, "wmq", after=[("kst", 0), ("kst", 1)])
            else:
                load_w(wmk, w_mk[l], KC, 512, "wmk", after=[("kst", 0), ("kst", 1)])
                load_w(wmv, w_mv[l], KC, 512, "wmv", after=[("kst", 0), ("kst", 1)])
                load_w(wmo, w_mo[l], 4, D, "wmo", after=[("kst", 0), ("kst", 1)])
        cur["offload"] = True
        otok = nc.alloc_sbuf_tensor_at("otok_%d" % l, [128, 4, 512], BF16, offset=sb.offs["wmk"])
        oT = nc.alloc_sbuf_tensor_at("oT_%d" % l, [128, 4, NT], BF16, offset=sb.offs["wmk"] + 4096)
        rden2 = sb.alloc("rden2", [128, 4], F32)
        ovs = [pO[i][:, 0:258].rearrange("p (q c) -> p q c", q=2) for i in range(2)]
        LOOK = 3
        if NTILES > 1:
            xload(1, xsrc)

        def transposes(src, dst_of, dkey_of):
            for cc in range(4):
                for qs in range(4):
                    P.op("pe", lambda e, cc=cc, qs=qs: e.transpose(
                        out=pT[:, qs * 128:(qs + 1) * 128], in_=src[:, qs, cc * 128:(cc + 1) * 128], identity=identb[:]),
                        reads=["att" if src is att else "otok", "identb"], writes=["pT"])
                P.op("dve", lambda e, cc=cc: e.tensor_copy(out=dst_of(cc), in_=pT[:, 0:NT]), reads=["pT"], writes=[dkey_of(cc)])

        def make_tail(j):
            s_ = j % 2
            xk = ("xt", s_)
            T0 = j * NT

            def s1():
                ck_ = lambda k: "catT_u" if k < 4 else ("catT", k - 4)
                proj_fm(wout, "wout", KC, 0, lambda k: catT[:, k, :], ck_, range(KC), l, evac_y)
                if j + 1 < NTILES:
                    uload(j + 1)
                if j >= 1 and j + 1 < NTILES:
                    xload(j + 1, xsrc)

            def s2():
                postnorm_residual(xt[s_], xk, l, 8)

            def s3():
                prenorm(xt[s_], xk, l, 16)

            def s4():
                def evac_qm(oc, bank, bkey):
                    P.op("dve", lambda e: e.tensor_copy(out=qm[:, oc, :], in_=bank[:, :]), reads=[bkey], writes=["qm"])
                proj_fm(wmq, "wmq", KC, 0, lambda k: hT[:, k, :], lambda k: ("hT", k), range(4), l, evac_qm)

            def s5():
                if j == 0:
                    mem_prep()
                msteps = [(hd, mc) for hd in range(4) for mc in range(2)]
                minfo = {}

                def emit_mS(i):
                    hd, mc = msteps[i]
                    bank, bk = nextB()
                    P.op("pe", lambda e, bank=bank, hd=hd, mc=mc: e.matmul(
                        bank[:, :], lhsT=memKT[:, hd, mc * 128:(mc + 1) * 128], rhs=qm[:, hd, :], start=True, stop=True),
                        reads=["memKT", "qm"], writes=[bk])
                    pi = ptc["i"] % 4
                    ptc["i"] += 1
                    ptt = pt[pi]
                    P.op("act", lambda e, bank=bank, ptt=ptt: e.activation(out=ptt[:], in_=bank[:, :], func=AF.Exp),
                         reads=[bk], writes=[("pt", pi)])
                    minfo[i] = pi

                def emit_mPV(i):
                    hd, mc = msteps[i]
                    pi = minfo[i]
                    ptt = pt[pi]
                    for qs in range(4):
                        bi = qs // 2
                        fst = (mc == 0 and qs % 2 == 0)
                        P.op("pe", lambda e, ptt=ptt, qs=qs, mc=mc, hd=hd, bi=bi, fst=fst: e.matmul(
                            ovs[bi][:, qs % 2, 0:129], lhsT=ptt[:, qs * 128:(qs + 1) * 128], rhs=memV[:, mc, hd, :],
                            start=fst, stop=(mc == 1 and qs % 2 == 1), skip_group_check=True),
                            reads=[("pt", pi), "memV"], writes=["pO%d" % bi])
                    if mc == 1:
                        for bi in range(2):
                            P.op("dve", lambda e, bi=bi: e.reciprocal(out=rden2[:, 2 * bi:2 * bi + 2], in_=ovs[bi][:, :, 128]),
                                 reads=["pO%d" % bi], writes=[("rden2", bi)])
                            P.op("dve", lambda e, bi=bi, hd=hd: e.tensor_tensor(
                                out=otok[:, 2 * bi:2 * bi + 2, hd * 128:(hd + 1) * 128], in0=ovs[bi][:, :, 0:128],
                                in1=rden2[:, 2 * bi:2 * bi + 2].unsqueeze(2).to_broadcast([128, 2, 128]), op=ALU.mult),
                                reads=["pO%d" % bi, ("rden2", bi)], writes=["otok"])

                for i in range(len(msteps) + 2):
                    if i < len(msteps):
                        emit_mS(i)
                    if i - 2 >= 0:
                        emit_mPV(i - 2)
            s5.needs_drain = True

            def s6():
                transposes(otok, lambda cc: oT[:, cc, :], lambda cc: ("oT", cc))
                proj_fm(wmo, "wmo", 4, 0, lambda k: oT[:, k, :], lambda k: ("oT", k), range(KC), l, evac_y)

            def s7():
                postnorm_residual(xt[s_], xk, l, 24)
                dma("pool", xdst.rearrange("(k p) t -> p k t", p=128)[:, :, T0:T0 + NT], xt[s_][:], ckeys(xk), [("scr", "xa")], "xs%d" % s_)
            return [s1, s2, s3, s4, s5, s6, s7]

        pend = []
        for j in range(NTILES):
            steps = []
            for h in range(8):
                nkb = 4 * j + 4
                for kb in range(nkb):
                    steps.append((h, kb, nkb))
            info = {}

            def emit_S(i):
                h, kb, nkb = steps[i]
                ks_ = kst[h % 2]
                kk = ("kst", h % 2)
                koff = 0
                if j == 0 and pre0:
                    ks_ = kst[0]
                    kk = ("kst", 0)
                    koff = h * NT
                if kb == 0 and j * 8 + h + 1 < len(seq):
                    kload(j * 8 + h + 1)
                if kb == 0 and j == 0 and h == 0:
                    wloads(0)
                    if NTILES == 1:
                        wloads(1)
                        wloads(2)
                if kb == 0 and j == 1 and h == 1:
                    wloads(1)
                    wloads(2)
                if kb == 0 and h == 3 and j >= 1 and j + 1 < NTILES and qslot(j + 1) != qslot(j):
                    qload(j + 1)
                dq = kb - 4 * j
                n0 = max(0, dq) * 128
                diag = dq >= 0
                bank, bk = nextB()
                qt_ = qts[qslot(j)]
                P.op("pe", lambda e, bank=bank, ks_=ks_, kb=kb, n0=n0, h=h, diag=diag, qt_=qt_, koff=koff: e.matmul(
                    bank[:, n0:NT], lhsT=ks_[0:70, koff + kb * 128:koff + (kb + 1) * 128], rhs=qt_[0:70, h, n0:NT],
                    start=True, stop=(not diag)), reads=[kk, ("qtile", qslot(j))], writes=[bk])
                if diag:
                    P.op("pe", lambda e, bank=bank, n0=n0: e.matmul(
                        bank[:, n0:n0 + 128], lhsT=identb[:], rhs=maskb[:], start=False, stop=True),
                        reads=["identb", "maskb"], writes=[bk])
                pi = ptc["i"] % 4
                ptc["i"] += 1
                ptt = pt[pi]
                P.op("act", lambda e, bank=bank, ptt=ptt, n0=n0: e.activation(out=ptt[:, n0:NT], in_=bank[:, n0:NT], func=AF.Exp),
                     reads=[bk], writes=[("pt", pi)])
                info[i] = (pi, n0)

            def emit_PV(i):
                h, kb, nkb = steps[i]
                pi, n0 = info[i]
                ptt = pt[pi]
                ob = pO[h % 2]
                okey = "pO%d" % (h % 2)
                ov = ob[:, 0:260].rearrange("p (q c) -> p q c", q=4)
                for qs in range(n0 // 128, 4):
                    first = (kb == 0 and qs == 0)
                    lastmm = (kb == nkb - 1) and (qs == 3)
                    P.op("pe", lambda e, ptt=ptt, qs=qs, kb=kb, h=h, first=first, lastmm=lastmm, ov=ov: e.matmul(
                        ov[:, qs, 0:65], lhsT=ptt[:, qs * 128:(qs + 1) * 128], rhs=Vaug[:, kb, h, :],
                        start=first, stop=lastmm, skip_group_check=True), reads=[("pt", pi), "Vaug"], writes=[okey])
                if kb == nkb - 1:
                    P.op("dve", lambda e, ov=ov: e.reciprocal(out=rden[:], in_=ov[:, :, 64]), reads=[okey], writes=["rden"])
                    P.op("dve", lambda e, ov=ov, h=h: e.tensor_tensor(
                        out=att[:, :, h * 64:(h + 1) * 64], in0=ov[:, :, 0:64], in1=rden[:].unsqueeze(2).to_broadcast([128, 4, 64]),
                        op=ALU.mult), reads=[okey, "rden"], writes=["att"])

            npv = 0
            for i in range(len(steps)):
                h, kb, nkb = steps[i]
                if kb == 0 and h >= 1 and pend:
                    st = pend.pop(0)
                    if getattr(st, "needs_drain", False):
                        while npv < i:
                            emit_PV(npv)
                            npv += 1
                    st()
                emit_S(i)
                while npv <= i - LOOK:
                    emit_PV(npv)
                    npv += 1
            while npv < len(steps):
                emit_PV(npv)
                npv += 1
            while pend:
                pend.pop(0)()
            if j + 1 < NTILES and (j == 0 or qslot(j + 1) == qslot(j)):
                qload(j + 1)
            transposes(att, lambda cc: catT[:, 4 + cc, :], lambda cc: ("catT", cc))
            pend = make_tail(j)
        while pend:
            pend.pop(0)()

        cur["offload"] = False
        P.barrier()
        sb.release(mark_c)
        P.canon = {}
        NT2 = 256
        N2 = S // NT2
        wup = sb.alloc("wup", [128, KC, DFF], BF16)
        wdn = sb.alloc("wdn", [128, 32, D], BF16)
        x2 = [sb.alloc("x2_%d" % i, [128, KC, NT2], F32) for i in range(3)]
        h2 = sb.alloc("h2", [128, KC, NT2], BF16)
        ysq2 = sb.alloc("ysq2", [128, KC, NT2], BF16)
        sqc = [sb.alloc("sqc%d" % i, [128, NT2], BF16) for i in range(8)]
        yr2 = sb.alloc("yr2", [128, KC, NT2], F32)
        rsa = sb.alloc("rsa", [128, NT2], F32)
        rsb = sb.alloc("rsb", [128, NT2], F32)
        rT = sb.alloc("rT", [128, 32, NT2], BF16)
        rtmp = [sb.alloc("rtmp%d" % i, [128, NT2], BF16) for i in range(4)]
        x2src = xa
        x2dst = yT if last else xb

        def x2load(jj):
            dma("sp", x2[jj % 3][:], x2src.rearrange("(k p) t -> p k t", p=128)[:, :, jj * NT2:(jj + 1) * NT2],
                [("scr", "xa")], [("x2", jj % 3)], "xl%d" % (jj % 3))

        def pre2(jj):
            xtile = x2[jj % 3]
            xkey = ("x2", jj % 3)
            for kc in range(KC):
                P.op("act", lambda e, kc=kc: e.activation(out=sqc[kc % 8][:], in_=xtile[:, kc, :], func=AF.Square),
                     reads=[xkey], writes=[("sqc", kc % 8)])
                P.op("pe", lambda e, kc=kc: e.matmul(pA[:, 0:NT2], lhsT=onesD[:], rhs=sqc[kc % 8][:],
                                                     start=(kc == 0), stop=(kc == KC - 1)),
                     reads=[("sqc", kc % 8), "onesD"], writes=["pA"])
            P.op("act", lambda e: e.activation(out=rsa[:], in_=pA[:, 0:NT2], func=AF.Ln, bias=epsc[:, 0:1]),
                 reads=["pA", "epsc"], writes=["rsa"])
            P.op("act", lambda e: e.activation(out=rsa[:], in_=rsa[:], func=AF.Exp, scale=-0.5), reads=["rsa"], writes=["rsa"])
            for kc in range(KC):
                P.op("dve", lambda e, kc=kc: e.scalar_tensor_tensor(
                    out=h2[:, kc, :], in0=xtile[:, kc, :], scalar=gcol(l, 40, kc), in1=rsa[:],
                    op0=ALU.mult, op1=ALU.mult), reads=[xkey, "rsa", "vec"], writes=[("h2", kc)])

        pend2 = []

        def up2(jj):
            for fc in range(32):
                if fc >= 4 and fc % 2 == 0 and pend2:
                    pend2.pop(0)()
                bank, bkey = nextB()
                for k in range(KC):
                    P.op("pe", lambda e, k=k, fc=fc, bank=bank: e.matmul(
                        bank[:, 0:NT2], lhsT=wup[:, k, fc * 128:(fc + 1) * 128], rhs=h2[:, k, :],
                        start=(k == 0), stop=(k == KC - 1)), reads=[("wup", fc // 8), ("h2", k)], writes=[bkey])
                rt = rtmp[fc % 4]
                P.op("act", lambda e, rt=rt, bank=bank: e.activation(out=rt[:], in_=bank[:, 0:NT2], func=AF.Relu),
                     reads=[bkey], writes=[("rtmp", fc % 4)])
                eng = "pool" if (fc % 4 == 1 and fc < 26) else "dve"
                P.op(eng, lambda e, rt=rt, fc=fc: e.tensor_tensor(out=rT[:, fc, :], in0=rt[:], in1=rt[:], op=ALU.mult),
                     reads=[("rtmp", fc % 4)], writes=[("rT", fc)])

        def down2(jj):
            for oc in range(KC):
                bank, bkey = nextB()
                for k in range(32):
                    P.op("pe", lambda e, k=k, oc=oc, bank=bank: e.matmul(
                        bank[:, 0:NT2], lhsT=wdn[:, k, oc * 128:(oc + 1) * 128], rhs=rT[:, k, :],
                        start=(k == 0), stop=(k == 31)), reads=[("wdn", oc // 2), ("rT", k)], writes=[bkey])
                P.op("act", lambda e, oc=oc, bank=bank: e.activation(out=yr2[:, oc, :], in_=bank[:, 0:NT2], func=AF.Copy),
                     reads=[bkey], writes=[("yr2", oc)])
                P.op("act", lambda e, oc=oc, bank=bank: e.activation(out=ysq2[:, oc, :], in_=bank[:, 0:NT2], func=AF.Square),
                     reads=[bkey], writes=[("ysq2", oc)])

        def post2_parts(jj):
            xtile = x2[jj % 3]
            xkey = ("x2", jj % 3)

            def head():
                for kc in range(KC):
                    P.op("pe", lambda e, kc=kc: e.matmul(pO[0][:, 0:NT2], lhsT=onesD[:], rhs=ysq2[:, kc, :],
                                                         start=(kc == 0), stop=(kc == KC - 1)),
                         reads=[("ysq2", kc), "onesD"], writes=["pO0"])
                P.op("act", lambda e: e.activation(out=rsb[:], in_=pO[0][:, 0:NT2], func=AF.Ln, bias=epsc[:, 0:1]),
                     reads=["pO0", "epsc"], writes=["rsb"])
                P.op("act", lambda e: e.activation(out=rsb[:], in_=rsb[:], func=AF.Exp, scale=-0.5), reads=["rsb"], writes=["rsb"])

            def pair(kc):
                def f():
                    P.op("dve", lambda e: e.scalar_tensor_tensor(
                        out=yr2[:, kc, :], in0=yr2[:, kc, :], scalar=gcol(l, 48, kc), in1=rsb[:],
                        op0=ALU.mult, op1=ALU.mult), reads=[("yr2", kc), "rsb", "vec"], writes=[("yr2", kc)])
                    P.op("pool", lambda e: e.tensor_tensor(out=xtile[:, kc, :], in0=xtile[:, kc, :], in1=yr2[:, kc, :],
                                                           op=ALU.add), reads=[xkey, ("yr2", kc)], writes=[xkey])
                return f

            def tail():
                dma("pool", x2dst.rearrange("(k p) t -> p k t", p=128)[:, :, jj * NT2:(jj + 1) * NT2], xtile[:], [xkey],
                    [("scr", "xb")], "xs%d" % (jj % 3))
                if jj + 3 < N2:
                    x2load(jj + 3)
            return [head] + [pair(kc) for kc in range(KC)] + [tail]

        x2load(0)
        load_w(wup, w_up[l], KC, DFF, "wup", after=[("x2", 0)], nkeys=2)
        load_w(wdn, w_down[l], 32, D, "wdn", step=256, after=[("x2", 0)], nkeys=2)
        pre2(0)
        if N2 > 1:
            x2load(1)
        if N2 > 2:
            x2load(2)
        for j in range(N2):
            up2(j)
            if j + 1 < N2:
                pre2(j + 1)
            down2(j)
            pend2.extend(post2_parts(j))
        while pend2:
            pend2.pop(0)()

    xsrc = xT
    for l in range(L):
        layer(l, xsrc)
        xsrc = xb

    P.barrier()

    sems = {e: nc.alloc_semaphore("s_" + e) for e in ENGINES}
    dsems = {k: nc.alloc_semaphore("d_" + k) for k in sorted(dma_keys)}
    run = P.emit(sems, dsems)
    with nc.Block() as block:
        @block.sync
        def _(e):
            run("sp", e)

        @block.scalar
        def _(e):
            run("act", e)

        @block.vector
        def _(e):
            run("dve", e)

        @block.gpsimd
        def _(e):
            run("pool", e)

        @block.tensor
        def _(e):
            run("pe", e)
    return nc, P, sb


def host_consts():
    c = np.zeros((128, 384), np.float32)
    c[:, 0:128] = np.eye(128, dtype=np.float32)
    s_idx = np.arange(128)[:, None]
    t_idx = np.arange(128)[None, :]
    c[:, 128:256] = np.where(s_idx > t_idx, -30000.0, 0.0).astype(np.float32)
    c[:, 256:384] = np.where(s_idx <= t_idx, 1.0, 0.0).astype(np.float32)
    return c


def pack_vecs(inp, L):
    v = np.zeros((128, L, NV), np.float32)

    def col8(a):
        return np.asarray(a, np.float32).reshape(8, 128).T

    def col4(a):
        return np.asarray(a, np.float32).reshape(4, 128).T

    for l in range(L):
        for i, nm in enumerate(["norm_mix_pre", "norm_mix_post", "norm_mem_pre", "norm_mem_post", "norm_memkv",
                                "norm_mlp_pre", "norm_mlp_post"]):
            v[:, l, 8 * i:8 * i + 8] = col8(inp[nm][l])
        v[:, l, 56:60] = col4(inp["conv_b"][l])
        v[:, l, 60:64] = col4(inp["conv_ln_g"][l])
        v[:, l, 64:68] = col4(inp["conv_ln_b"][l])
        v[0:8, l, 68] = np.asarray(inp["b_forget"][l], np.float32)
        v[:, l, NVB:NVB + 8] = np.asarray(inp["b_forget"][l], np.float32)[None, :]
        cw = np.asarray(inp["conv_w"][l], np.float32)
        v[:, l, 69:69 + 124] = cw.T.reshape(4, 128, 31).transpose(1, 0, 2).reshape(128, 124)
    return v


_CACHE = {}


def run_device(inp, S, L, B, dbg=False):
    key = (S, L, dbg)
    if key not in _CACHE:
        _CACHE[key] = build(S, L, dbg)
    nc = _CACHE[key][0]
    shared = {
        "w_in": np.ascontiguousarray(inp["w_in"][:L], np.float32),
        "w_out": np.ascontiguousarray(inp["w_out"][:L], np.float32),
        "w_mq": np.ascontiguousarray(inp["w_mq"][:L], np.float32),
        "w_mk": np.ascontiguousarray(inp["w_mk"][:L], np.float32),
        "w_mv": np.ascontiguousarray(inp["w_mv"][:L], np.float32),
        "w_mo": np.ascontiguousarray(inp["w_mo"][:L], np.float32),
        "w_up": np.ascontiguousarray(inp["w_up"][:L], np.float32),
        "w_down": np.ascontiguousarray(inp["w_down"][:L], np.float32),
        "vecs": pack_vecs(inp, L),
        "cst": host_consts(),
    }
    x = np.asarray(inp["x"], np.float32)
    mem = np.asarray(inp["mem"], np.float32)
    in_maps = []
    for b in range(B):
        m = dict(shared)
        m["xT"] = np.ascontiguousarray(x[b].T)
        m["memT"] = np.ascontiguousarray(mem[b].T)
        in_maps.append(m)
    res = run_bass_kernel_spmd(nc, in_maps, core_ids=list(range(B)))
    return res


def kernel(**inputs):
    B, S, _ = inputs["x"].shape
    L = inputs["w_in"].shape[0]
    res = run_device(inputs, S, L, B)
    out = np.empty((B, S, D), np.float32)
    for b in range(B):
        out[b] = res.results[b]["yT"].T
    return out
```
